# Optimizing a Trainium2 kernel written in Bass

```python
import jax, jax.numpy as jnp
from jax import lax
import numpy as np

D_MODEL = 4096
BATCH = 4
SEQ = 4096
DEPTH = 1

MEM_LEN = 256
MIX_WIDTH = D_MODEL
GROUP_DIM = 128
CONV_WIDTH = MIX_WIDTH // 2
SCONV_WIDTH = MIX_WIDTH - CONV_WIDTH
N_CONV_GROUPS = CONV_WIDTH // GROUP_DIM
N_SCONV_HEADS = SCONV_WIDTH // GROUP_DIM
IN_WIDTH = 2 * CONV_WIDTH + 3 * SCONV_WIDTH
CONV_KERNEL = 31
SHORT_KERNEL = 3
FFN_KERNEL = 3
D_FF = ((8 * D_MODEL // 3 + 255) // 256) * 256
N_XATTN_HEADS = 4
XATTN_HEAD_DIM = D_MODEL // N_XATTN_HEADS
EPS = 1e-6

kernel_name = "hybrid_conformer_shortconv_xattn_convffn"


def rms_norm(x, g):
    xf = x.astype(jnp.float32)
    y = xf * lax.rsqrt(jnp.mean(xf * xf, axis=-1, keepdims=True) + EPS)
    return (y * g.astype(jnp.float32)).astype(x.dtype)


def group_layer_norm(x, g, b, group):
    shape = x.shape
    xf = x.astype(jnp.float32).reshape(*shape[:-1], shape[-1] // group, group)
    mu = jnp.mean(xf, axis=-1, keepdims=True)
    xc = xf - mu
    var = jnp.mean(xc * xc, axis=-1, keepdims=True)
    y = (xc * lax.rsqrt(var + EPS)).reshape(shape)
    return (y * g.astype(jnp.float32) + b.astype(jnp.float32)).astype(x.dtype)


def causal_dwconv(x, w):
    K = w.shape[0]
    T = x.shape[1]
    xp = jnp.pad(x, ((0, 0), (K - 1, 0), (0, 0)))
    y = xp[:, 0:T, :] * w[0]
    for k in range(1, K):
        y = y + xp[:, k:k + T, :] * w[k]
    return y


def setup_inputs(seed: int = 0) -> dict:
    key = jax.random.key(seed)
    ks = jax.random.split(key, 24)

    def nrm(k, shape, scale):
        return jax.random.normal(k, shape, jnp.float32) * scale

    def gain(k, shape):
        return 1.0 + 0.02 * jax.random.normal(k, shape, jnp.float32)

    L = DEPTH
    return {
        "x": nrm(ks[0], (BATCH, SEQ, D_MODEL), 1.0),
        "mem": nrm(ks[1], (BATCH, MEM_LEN, D_MODEL), 1.0),
        "g_mix": gain(ks[2], (L, D_MODEL)),
        "w_in": nrm(ks[3], (L, D_MODEL, IN_WIDTH), D_MODEL ** -0.5),
        "conv_a_w": nrm(ks[4], (L, CONV_KERNEL, CONV_WIDTH), CONV_KERNEL ** -0.5),
        "conv_a_b": nrm(ks[5], (L, CONV_WIDTH), 0.02),
        "ln_a_g": gain(ks[6], (L, CONV_WIDTH)),
        "ln_a_b": nrm(ks[7], (L, CONV_WIDTH), 0.02),
        "conv_b_w": nrm(ks[8], (L, SHORT_KERNEL, SCONV_WIDTH), SHORT_KERNEL ** -0.5),
        "w_out": nrm(ks[9], (L, MIX_WIDTH, D_MODEL), MIX_WIDTH ** -0.5),
        "g_xattn": gain(ks[10], (L, D_MODEL)),
        "g_mem": gain(ks[11], (D_MODEL,)),
        "w_q": nrm(ks[12], (L, D_MODEL, D_MODEL), D_MODEL ** -0.5),
        "w_k": nrm(ks[13], (L, D_MODEL, D_MODEL), D_MODEL ** -0.5),
        "w_v": nrm(ks[14], (L, D_MODEL, D_MODEL), D_MODEL ** -0.5),
        "w_o": nrm(ks[15], (L, D_MODEL, D_MODEL), D_MODEL ** -0.5),
        "g_ffn": gain(ks[16], (L, D_MODEL)),
        "w_gate": nrm(ks[17], (L, D_MODEL, D_FF), D_MODEL ** -0.5),
        "w_up": nrm(ks[18], (L, D_MODEL, D_FF), D_MODEL ** -0.5),
        "conv_f_w": nrm(ks[19], (L, FFN_KERNEL, D_FF), FFN_KERNEL ** -0.5),
        "w_down": nrm(ks[20], (L, D_FF, D_MODEL), D_FF ** -0.5),
        "g_final": gain(ks[21], (D_MODEL,)),
    }


def reference(x, mem, g_mix, w_in, conv_a_w, conv_a_b, ln_a_g, ln_a_b, conv_b_w,
              w_out, g_xattn, g_mem, w_q, w_k, w_v, w_o, g_ffn, w_gate, w_up,
              conv_f_w, w_down, g_final):
    B, T, D = x.shape
    M = mem.shape[1]
    memn = rms_norm(mem, g_mem)
    splits = [CONV_WIDTH, 2 * CONV_WIDTH, 2 * CONV_WIDTH + SCONV_WIDTH,
              2 * CONV_WIDTH + 2 * SCONV_WIDTH]
    h = x
    for l in range(DEPTH):
        xn = rms_norm(h, g_mix[l])
        proj = xn @ w_in[l]
        a_val, a_gate, b_gate, c_gate, b_h = jnp.split(proj, splits, axis=-1)
        u = a_val * jax.nn.sigmoid(a_gate)
        u = causal_dwconv(u, conv_a_w[l]) + conv_a_b[l]
        u = jax.nn.silu(group_layer_norm(u, ln_a_g[l], ln_a_b[l], GROUP_DIM))
        v = b_gate * causal_dwconv(c_gate * b_h, conv_b_w[l])
        mix = jnp.concatenate([u, v], axis=-1)
        h = h + mix @ w_out[l]

        xn = rms_norm(h, g_xattn[l])
        q = (xn @ w_q[l]).reshape(B, T, N_XATTN_HEADS, XATTN_HEAD_DIM)
        k = (memn @ w_k[l]).reshape(B, M, N_XATTN_HEADS, XATTN_HEAD_DIM)
        vm = (memn @ w_v[l]).reshape(B, M, N_XATTN_HEADS, XATTN_HEAD_DIM)
        s = jnp.einsum('bthd,bmhd->bhtm', q.astype(jnp.float32), k.astype(jnp.float32))
        p = jax.nn.softmax(s * (XATTN_HEAD_DIM ** -0.5), axis=-1).astype(vm.dtype)
        o = jnp.einsum('bhtm,bmhd->bthd', p, vm).reshape(B, T, D)
        h = h + o @ w_o[l]

        xn = rms_norm(h, g_ffn[l])
        g = causal_dwconv(xn @ w_gate[l], conv_f_w[l])
        f = jax.nn.silu(g) * (xn @ w_up[l])
        h = h + f @ w_down[l]
    return rms_norm(h, g_final)
```

```python
import numpy as np
from contextlib import ExitStack
import concourse.bass as bass
import concourse.mybir as mybir
from concourse.bass_utils import run_bass_kernel_spmd

F32 = mybir.dt.float32
BF16 = mybir.dt.bfloat16
ALU = mybir.AluOpType
AF = mybir.ActivationFunctionType
AX = mybir.AxisListType

D = 4096
DC = 32
SEQ = 4096
TOK_CORE = 2048
HALO = 32
NTM = 512
NTC = NTM + HALO
MEM = 256
INW = 10240
DFF = 11008
FC = 86
EPS = 1e-6
NW = 4
NSCR = 12
NSTG = 3
SCRW = NTC + 32

P_GMIX, P_GX, P_GF, P_GFIN, P_GMEM = 0, 32, 64, 96, 128
P_CAW = 160
P_CAB = P_CAW + 16 * 31
P_LNG = P_CAB + 16
P_LNB = P_LNG + 16
P_CBW = P_LNB + 16
P_CFW = P_CBW + 48
P_MASK = P_CFW + FC * 3
P_EPS = P_MASK + 1
P_ID = P_MASK + 2
NPAR = P_ID + 128


class T:
    __slots__ = ("name", "w", "r", "dsem", "dcount")

    def __init__(self, name):
        self.name = name
        self.w = None
        self.r = {}
        self.dsem = None
        self.dcount = 0


class Eng:
    def __init__(self, name):
        self.name = name
        self.count = 0
        self.ops = []
        self.waited = {}
        self.pending = False


class Prog:
    def __init__(self):
        self.engs = {n: Eng(n) for n in ("pe", "act", "dve", "pool", "sp")}
        self.dsems = []

    def _deps(self, eng, reads, writes):
        need = {}
        def add(ev):
            if ev is None:
                return
            sk, v = ev
            if eng.waited.get(sk, 0) < v and need.get(sk, 0) < v:
                need[sk] = v
        for t in reads:
            add(t.w)
        for t in writes:
            add(t.w)
            for sk, v in t.r.items():
                add((sk, v))
        for sk, v in need.items():
            eng.waited[sk] = v
        return list(need.items())

    def op(self, en, fn, reads=(), writes=(), signal=True, sepwait=False):
        eng = self.engs[en]
        waits = self._deps(eng, reads, writes)
        ev = (en, eng.count + 1)
        if signal:
            eng.count += 1
            eng.pending = False
        else:
            eng.pending = True
        eng.ops.append(("op", fn, waits, signal, sepwait))
        for t in reads:
            if t.r.get(ev[0], 0) < ev[1]:
                t.r[ev[0]] = ev[1]
        for t in writes:
            t.w = ev
            t.r = {}
        return ev

    def dma(self, en, fn, semT, reads=(), writes=()):
        eng = self.engs[en]
        waits = self._deps(eng, reads, writes)
        if semT.dsem is None:
            semT.dsem = "d%d" % len(self.dsems)
            self.dsems.append(semT.dsem)
        semT.dcount += 1
        ev = (semT.dsem, 16 * semT.dcount)
        eng.ops.append(("dma", fn, waits, semT.dsem, True))
        for t in reads:
            if t.r.get(ev[0], 0) < ev[1]:
                t.r[ev[0]] = ev[1]
        for t in writes:
            t.w = ev
            t.r = {}
        return ev

    def wait_all(self, en, tiles):
        eng = self.engs[en]
        waits = self._deps(eng, [], tiles)
        eng.ops.append(("wait", None, waits, None, True))

    def emit(self, nc, es):
        sems = {}
        for n in list(self.engs) + self.dsems:
            sems[n] = es.enter_context(nc.semaphore("s_" + n))
        block = es.enter_context(nc.Block())
        hw = {"pe": block.tensor, "act": block.scalar, "dve": block.vector,
              "pool": block.gpsimd, "sp": block.sync}

        def make(en):
            eng = self.engs[en]

            def body(e):
                for kind, fn, waits, sig, sepwait in eng.ops:
                    if kind == "wait":
                        for sk, v in waits:
                            e.wait_ge(sems[sk], v)
                        continue
                    if sepwait or len(waits) == 0:
                        for sk, v in waits:
                            e.wait_ge(sems[sk], v)
                        first, last = fn(e)
                    else:
                        for sk, v in waits[1:]:
                            e.wait_ge(sems[sk], v)
                        first, last = fn(e)
                        first._wait_ge(sems[waits[0][0]], waits[0][1])
                    if kind == "dma":
                        last.then_inc(sems[sig], 16)
                    elif sig:
                        last.then_inc(sems[en], 1)
            return body
        for en in self.engs:
            if self.engs[en].ops:
                hw[en](make(en))


class FL:
    def __init__(self):
        self.f = None
        self.l = None

    def __call__(self, ins):
        if self.f is None:
            self.f = ins
        self.l = ins
        return ins


def build(ntiles=4, upto=4, dbg=False):
    nc = bass.Bass("TRN2", target_bir_lowering=False)
    xin = nc.dram_tensor("xin", [HALO + TOK_CORE, D], F32, kind="ExternalInput").ap()
    memin = nc.dram_tensor("memin", [MEM, D], F32, kind="ExternalInput").ap()
    par_d = nc.dram_tensor("params", [128, NPAR], F32, kind="ExternalInput").ap()
    wd = {}
    for name, shp in (("w_in", [D, INW]), ("w_out", [D, D]), ("w_q", [D, D]), ("w_k", [D, D]),
                      ("w_v", [D, D]), ("w_o", [D, D]), ("w_gate", [D, DFF]), ("w_up", [D, DFF]),
                      ("w_down", [DFF, D])):
        wd[name] = nc.dram_tensor(name, shp, F32, kind="ExternalInput").ap().rearrange("(k p) c -> p k c", p=128)
    out_d = nc.dram_tensor("out", [TOK_CORE, D], F32, kind="ExternalOutput").ap()
    kt_scr = nc.dram_tensor("kt_scr", [4, 128, 8 * MEM], BF16, kind="Internal").ap()
    v_scr = nc.dram_tensor("v_scr", [4, 128, 2 * 1024], BF16, kind="Internal").ap()

    pg = Prog()
    es = ExitStack()
    es.enter_context(nc.allow_low_precision("bf16 matmul operands with fp32 PSUM accumulation"))

    def sb(name, shape, dt):
        return es.enter_context(nc.sbuf_tensor(name, shape, dt))

    h = sb("h", [128, DC, NTC], F32)
    xn = sb("xn", [128, DC, NTC], BF16)
    grp = [sb("grp%d" % i, [128, 8, NTC], BF16) for i in range(2)]
    wsl = [sb("wsl%d" % i, [128, 4096], BF16) for i in range(NW)]
    ktb = sb("ktb", [128, 8, MEM], BF16)
    vb = sb("vb", [128, 2, 1024], BF16)
    par = sb("par", [128, NPAR], F32)
    identB = sb("identB", [128, 128], BF16)
    onesB = sb("onesB", [128, 128], BF16)
    onesF = sb("onesF", [128, 128], F32)
    sqbf = [sb("sqbf%d" % i, [128, NTC], BF16) for i in range(2)]
    scr = [sb("scr%d" % i, [128, SCRW], F32) for i in range(NSCR)]
    rstd = sb("rstd", [128, NTC], F32)
    uhist = sb("uhist", [128, 16, 30], F32)
    cbhist = sb("cbhist", [128, 16, 2], F32)
    ghist = sb("ghist", [128, FC, 2], F32)
    stg = [sb("stg%d" % i, [128, 512], F32) for i in range(NSTG)]
    Pt = [sb("Pt%d" % i, [128, MEM], BF16) for i in range(2)]
    PTsb = sb("PTsb", [128, 2, NTC], BF16)
    small = [sb("small%d" % i, [128, 4], F32) for i in range(4)]
    ps = es.enter_context(nc.psum_tensor("ps", [128, 8, 512], F32))

    hT = [T("h%d" % c) for c in range(DC)]
    xnT = [T("xn%d" % c) for c in range(DC)]
    grpT = [[T("g%d_%d" % (i, k)) for k in range(8)] for i in range(2)]
    wT = [T("w%d" % i) for i in range(NW)]
    ktT, vT, parT = T("kt"), T("v"), T("par")
    constT = T("const")
    sqT = [T("sq%d" % i) for i in range(2)]
    scrT = [T("scr%d" % i) for i in range(NSCR)]
    rstdT = T("rstd")
    uhT = [T("uh%d" % j) for j in range(16)]
    cbhT = [T("cbh%d" % j) for j in range(16)]
    ghT = [T("gh%d" % j) for j in range(FC)]
    stgT = [T("stg%d" % i) for i in range(NSTG)]
    stg = stg + [rstd]
    stgT = stgT + [rstdT]
    PtT = [T("Pt%d" % i) for i in range(2)]
    PTsbT = T("PTsb")
    smallT = [T("sm%d" % i) for i in range(4)]
    accT = [T("acc%d" % i) for i in range(8)]
    ktscrT = [T("ktscr%d" % i) for i in range(4)]
    vscrT = [T("vscr%d" % i) for i in range(4)]

    cnt = {"acc": 0, "w": 0, "stg": 0, "sq": 0, "pt": 0, "sm": 0}

    def nxt(key, n):
        i = cnt[key] % n
        cnt[key] += 1
        return i

    cur = {"nacc": 4, "hold": set(), "pre_acc": None}

    def nacc():
        while True:
            i = nxt("acc", cur["nacc"])
            if i not in cur["hold"]:
                return i

    def acc_ap(ai, kind, n, c0=0):
        if kind == "h":
            return ps[:, 4 + ai, c0:c0 + n]
        return ps[:, ai, c0:c0 + n]

    def pc(col):
        return par[:, col:col + 1]

    def wload(view_parts):
        si = nxt("w", NW)
        for (lo, hi, src, shape3) in view_parts:
            def fn(e, lo=lo, hi=hi, src=src, shape3=shape3, si=si):
                dst = wsl[si][:, lo:hi].rearrange("p (k c) -> p k c", c=shape3)
                i = e.dma_start(out=dst, in_=src)
                return i, i
            pg.dma("pool", fn, wT[si], reads=(), writes=(wT[si],))
        return si

    def fullk(wname, col0, blocks, ncols, rhs_chunk, rhs_tiles, n_acc=2):
        accs = [nacc() for _ in range(n_acc)]
        for kh in range(2):
            si = wload([(0, 4096, wd[wname][:, kh * 16:(kh + 1) * 16, col0:col0 + 256], 256)])

            for oi in range(n_acc):
                def fn(e, kh=kh, si=si, accs=accs, oi=oi):
                    fl = FL()
                    wv = wsl[si][:, :].rearrange("p (k c) -> p k c", c=256)
                    for kk in range(16):
                        k = kh * 16 + kk
                        for (c0, n, kind) in blocks:
                            fl(e.matmul(acc_ap(accs[oi], kind, n), lhsT=wv[:, kk, oi * 128:(oi + 1) * 128],
                                        rhs=rhs_chunk(k, c0, n), start=(k == 0), stop=(k == 31)))
                    return fl.f, fl.l
                pg.op("pe", fn, reads=[wT[si]] + rhs_tiles[kh * 16:(kh + 1) * 16], writes=[accT[accs[oi]]])
        return accs

    def partial_proj(wname, rowsets, gi, nk, blocks):
        for ob in range(8):
            parts = []
            lo = 0
            for (r0, rn) in rowsets:
                parts.append((lo, lo + rn * 512, wd[wname][:, r0:r0 + rn, ob * 512:(ob + 1) * 512], 512))
                lo += rn * 512
            si = wload(parts)
            accs = [nacc() for _ in range(4)]

            for i in range(4):
                def fn(e, si=si, accs=accs, i=i):
                    fl = FL()
                    wv = wsl[si][:, 0:nk * 512].rearrange("p (k c) -> p k c", c=512)
                    for kk in range(nk):
                        for (c0, n, kind) in blocks:
                            fl(e.matmul(acc_ap(accs[i], kind, n), lhsT=wv[:, kk, i * 128:(i + 1) * 128],
                                        rhs=grp[gi][:, kk, c0:c0 + n], start=(kk == 0), stop=(kk == nk - 1)))
                    return fl.f, fl.l
                pg.op("pe", fn, reads=[wT[si]] + grpT[gi][:nk], writes=[accT[accs[i]]])
            for i in range(4):
                oc = ob * 4 + i
                for (c0, n, kind) in blocks:
                    def fn2(e, oc=oc, c0=c0, n=n, kind=kind, a=accs[i]):
                        ins = e.tensor_tensor(out=h[:, oc, c0:c0 + n], in0=h[:, oc, c0:c0 + n],
                                              in1=acc_ap(a, kind, n), op=ALU.add)
                        return ins, ins
                    pg.op("dve", fn2, reads=[accT[accs[i]]], writes=[hT[oc]])

    def rms_stat_chunk(a, c, blocks, ncol):
        qi = nxt("sq", 2)

        def fn(e, c=c, qi=qi):
            ins = e.activation(out=sqbf[qi][:, 0:ncol], in_=h[:, c, 0:ncol], func=AF.Square)
            return ins, ins
        pg.op("act", fn, reads=[hT[c]], writes=[sqT[qi]])

        def fn2(e, c=c, qi=qi, a=a):
            fl = FL()
            for (c0, n, kind) in blocks:
                fl(e.matmul(acc_ap(a, kind, n), lhsT=onesB[:, :], rhs=sqbf[qi][:, c0:c0 + n],
                            start=(c == 0), stop=(c == DC - 1)))
            return fl.f, fl.l
        pg.op("pe", fn2, reads=[sqT[qi], constT], writes=[accT[a]])

    def rmsnorm(gcol, blocks, ncol, final=False, pre_acc=None):
        if pre_acc is None:
            a = nacc()
            for c in range(DC):
                rms_stat_chunk(a, c, blocks, ncol)
        else:
            a = pre_acc
        for (c0, n, kind) in blocks:
            def fn3(e, c0=c0, n=n, kind=kind, a=a):
                ins = e.activation(out=rstd[:, c0:c0 + n], in_=acc_ap(a, kind, n), func=AF.Sqrt, bias=pc(P_EPS))
                return ins, ins
            pg.op("act", fn3, reads=[accT[a], parT], writes=[rstdT])

            def fn3b(e, c0=c0, n=n):
                ins = e.reciprocal(out=rstd[:, c0:c0 + n], in_=rstd[:, c0:c0 + n])
                return ins, ins
            pg.op("dve", fn3b, reads=[], writes=[rstdT])
        cur["hold"].discard(a)
        for c in range(DC):
            def fn4(e, c=c):
                dst = h[:, c, 0:ncol] if final else xn[:, c, 0:ncol]
                ins = e.scalar_tensor_tensor(out=dst, in0=h[:, c, 0:ncol], scalar=pc(gcol + c),
                                             in1=rstd[:, 0:ncol], op0=ALU.mult, op1=ALU.mult)
                return ins, ins
            pg.op("dve", fn4, reads=[hT[c], rstdT, parT], writes=[hT[c] if final else xnT[c]])

    def load_rows(src_rows_ap, nrows, dst_cols, dstT_list, dst_is_h=True):
        for db in range(8):
            load_unit(src_rows_ap, nrows, dst_cols, db)

    def load_unit(src_rows_ap, nrows, dst_cols, db):
        if True:
            si = nxt("stg", NSTG + 1)

            def fn(e, db=db, si=si):
                ins = e.dma_start(out=stg[si][0:nrows, 0:512], in_=src_rows_ap[:, db * 512:(db + 1) * 512])
                return ins, ins
            pg.dma("sp", fn, stgT[si], reads=(), writes=(stgT[si],))
            a = nacc()

            def fn2(e, si=si, a=a):
                fl = FL()
                for i in range(4):
                    fl(e.transpose(ps[:, a, i * nrows:(i + 1) * nrows], stg[si][0:nrows, i * 128:(i + 1) * 128],
                                       par[0:nrows, P_ID:P_ID + nrows]))
                return fl.f, fl.l
            pg.op("pe", fn2, reads=[stgT[si], constT], writes=[accT[a]])

            def fn3(e, db=db, a=a):
                src = ps[:, a, 0:4 * nrows].rearrange("p (i t) -> p i t", t=nrows)
                ins = e.activation(out=h[:, db * 4:(db + 1) * 4, dst_cols:dst_cols + nrows], in_=src, func=AF.Copy)
                return ins, ins
            pg.op("act", fn3, reads=[accT[a]], writes=hT[db * 4:(db + 1) * 4])

    def f_par(e):
        ins = e.dma_start(out=par[:, :], in_=par_d)
        return ins, ins
    pg.dma("sp", f_par, parT, writes=(parT,))

    def f_init(e):
        e.memset(onesB[:, :], 1.0 / 4096.0)
        e.memset(onesF[:, :], 1.0 / 128.0)
        e.memset(uhist[:, :, :], 0.0)
        e.memset(cbhist[:, :, :], 0.0)
        ins = e.memset(ghist[:, :, :], 0.0)
        return ins, ins
    pg.op("dve", f_init, writes=[constT] + uhT + cbhT + ghT)

    def f_ident(e):
        ins = e.activation(out=identB[:, :], in_=par[:, P_ID:P_ID + 128], func=AF.Copy)
        return ins, ins
    pg.op("act", f_ident, reads=[parT], writes=[constT])

    mblocks = [(0, MEM, "m")]
    if upto >= 2:
        for mc in range(2):
            load_rows(memin[mc * 128:(mc + 1) * 128, :], 128, mc * 128, hT)
        rmsnorm(P_GMEM, mblocks, MEM)
        for cb in range(16):
            accs = fullk("w_k", cb * 256, mblocks, MEM, lambda k, c0, n: xn[:, k, c0:c0 + n], xnT)
            for oi in range(2):
                oc = cb * 2 + oi
                gi, kk = (oc // 8) % 2, oc % 8

                def fn(e, a=accs[oi], gi=gi, kk=kk):
                    ins = e.activation(out=grp[gi][:, kk, 0:MEM], in_=ps[:, a, 0:MEM], func=AF.Copy)
                    return ins, ins
                pg.op("act", fn, reads=[accT[accs[oi]]], writes=[grpT[gi][kk]])
            if cb % 4 == 3:
                hd = cb // 4
                gi = hd % 2

                def fn(e, hd=hd, gi=gi):
                    ins = e.dma_start(out=kt_scr[hd].rearrange("p (k m) -> p k m", m=MEM), in_=grp[gi][:, :, 0:MEM])
                    return ins, ins
                pg.dma("sp", fn, ktscrT[hd], reads=grpT[gi], writes=(ktscrT[hd],))
        for cb in range(16):
            accs = [nacc() for _ in range(2)]
            for kh in range(2):
                si = wload([(0, 4096, wd["w_v"][:, kh * 16:(kh + 1) * 16, cb * 256:(cb + 1) * 256], 256)])

                def fn(e, kh=kh, si=si, accs=accs):
                    fl = FL()
                    wv = wsl[si][:, :].rearrange("p (k c) -> p k c", c=256)
                    for mc in range(2):
                        for kk in range(16):
                            k = kh * 16 + kk
                            fl(e.matmul(ps[:, accs[mc], 0:256], lhsT=xn[:, k, mc * 128:(mc + 1) * 128],
                                            rhs=wv[:, kk, :], start=(k == 0), stop=(k == 31)))
                    return fl.f, fl.l
                pg.op("pe", fn, reads=[wT[si]] + xnT[kh * 16:(kh + 1) * 16], writes=[accT[a] for a in accs])
            hd = cb // 4
            gi = hd % 2
            for mc in range(2):
                def fn(e, a=accs[mc], gi=gi, mc=mc, cb=cb):
                    flat = grp[gi][:, :, :].rearrange("p k c -> p (k c)")
                    off = mc * 1024 + (cb % 4) * 256
                    ins = e.activation(out=flat[:, off:off + 256], in_=ps[:, a, 0:256], func=AF.Copy)
                    return ins, ins
                pg.op("act", fn, reads=[accT[accs[mc]]], writes=grpT[gi])
            if cb % 4 == 3:
                def fn(e, hd=hd, gi=gi):
                    flat = grp[gi][:, :, :].rearrange("p k c -> p (k c)")
                    ins = e.dma_start(out=v_scr[hd], in_=flat[:, 0:2048])
                    return ins, ins
                pg.dma("sp", fn, vscrT[hd], reads=grpT[gi], writes=(vscrT[hd],))

    def do_tile(ti):
        if ti == 0:
            blocks = [(0, HALO, "h"), (HALO, NTM, "m")]
            ncol = NTC
            row0 = 0
        else:
            blocks = [(0, NTM, "m")]
            ncol = NTM
            row0 = HALO + ti * NTM
            if ti == 1:
                for i in range(4):
                    accT[4 + i].w = accT[i].w
                    accT[4 + i].r = dict(accT[i].r)
                cur["nacc"] = 8
                cnt["acc"] = 0
        mainc0 = blocks[-1][0]

        pre_acc = cur["pre_acc"]
        cur["pre_acc"] = None
        if ti == 0:
            load_rows(xin[0:HALO, :], HALO, 0, hT)
            for tc in range(4):
                load_rows(xin[HALO + tc * 128:HALO + (tc + 1) * 128, :], 128, mainc0 + tc * 128, hT)

        if upto >= 1:
            rmsnorm(P_GMIX, blocks, ncol, pre_acc=pre_acc)
            rhs_xn = lambda k, c0, n: xn[:, k, c0:c0 + n]
            pend_back = []
            pend_proj = []

            def front(p):
                av = fullk("w_in", p * 256, blocks, ncol, rhs_xn, xnT)
                ag = fullk("w_in", 2048 + p * 256, blocks, ncol, rhs_xn, xnT)
                ub = []
                for oi in range(2):
                    j = 2 * p + oi
                    sg = oi
                    u = 2 + (2 * p + oi) % 4
                    ub.append(u)
                    for (c0, n, kind) in blocks:
                        def fn(e, a=ag[oi], sg=sg, c0=c0, n=n, kind=kind):
                            ins = e.activation(out=scr[sg][:, c0:c0 + n], in_=acc_ap(a, kind, n), func=AF.Sigmoid)
                            return ins, ins
                        pg.op("act", fn, reads=[accT[ag[oi]]], writes=[scrT[sg]])

                    def fnh(e, j=j, u=u):
                        ins = e.activation(out=scr[u][:, 0:30], in_=uhist[:, j, :], func=AF.Copy)
                        return ins, ins
                    pg.op("act", fnh, reads=[uhT[j]], writes=[scrT[u]])
                    for (c0, n, kind) in blocks:
                        def fn(e, a=av[oi], sg=sg, u=u, c0=c0, n=n, kind=kind):
                            ins = e.tensor_tensor(out=scr[u][:, 30 + c0:30 + c0 + n], in0=scr[sg][:, c0:c0 + n],
                                                  in1=acc_ap(a, kind, n), op=ALU.mult)
                            return ins, ins
                        pg.op("dve", fn, reads=[accT[av[oi]], scrT[sg]], writes=[scrT[u]])

                    def fnh2(e, j=j, u=u):
                        ins = e.activation(out=uhist[:, j, :], in_=scr[u][:, ncol:ncol + 30], func=AF.Copy)
                        return ins, ins
                    pg.op("act", fnh2, reads=[scrT[u]], writes=[uhT[j]])
                yield ub
                cg = fullk("w_in", 6144 + p * 256, blocks, ncol, rhs_xn, xnT)
                bh = fullk("w_in", 8192 + p * 256, blocks, ncol, rhs_xn, xnT)
                for oi in range(2):
                    j = 2 * p + oi
                    th = oi
                    cbi = 6 + oi
                    for (c0, n, kind) in blocks:
                        def fn(e, a=bh[oi], th=th, c0=c0, n=n, kind=kind):
                            ins = e.activation(out=scr[th][:, c0:c0 + n], in_=acc_ap(a, kind, n), func=AF.Copy)
                            return ins, ins
                        pg.op("act", fn, reads=[accT[bh[oi]]], writes=[scrT[th]])

                    def fnh(e, j=j, cbi=cbi):
                        ins = e.activation(out=scr[cbi][:, 0:2], in_=cbhist[:, j, :], func=AF.Copy)
                        return ins, ins
                    pg.op("act", fnh, reads=[cbhT[j]], writes=[scrT[cbi]])
                    for (c0, n, kind) in blocks:
                        def fn(e, a=cg[oi], th=th, cbi=cbi, c0=c0, n=n, kind=kind):
                            ins = e.tensor_tensor(out=scr[cbi][:, 2 + c0:2 + c0 + n], in0=scr[th][:, c0:c0 + n],
                                                  in1=acc_ap(a, kind, n), op=ALU.mult)
                            return ins, ins
                        pg.op("dve", fn, reads=[accT[cg[oi]], scrT[th]], writes=[scrT[cbi]])

                    def fnh2(e, j=j, cbi=cbi):
                        ins = e.activation(out=cbhist[:, j, :], in_=scr[cbi][:, ncol:ncol + 2], func=AF.Copy)
                        return ins, ins
                    pg.op("act", fnh2, reads=[scrT[cbi]], writes=[cbhT[j]])
                yield None
                bg = fullk("w_in", 4096 + p * 256, blocks, ncol, rhs_xn, xnT)
                for oi in range(2):
                    j = 2 * p + oi
                    cbi = 6 + oi
                    vt = oi
                    col = P_CBW + j * 3

                    def fn(e, cbi=cbi, vt=vt, col=col):
                        first = e.tensor_scalar(out=scr[vt][:, 0:ncol], in0=scr[cbi][:, 0:ncol], scalar1=pc(col),
                                                scalar2=None, op0=ALU.mult)
                        return first, first
                    pg.op("dve", fn, reads=[scrT[cbi], parT], writes=[scrT[vt]])
                    for k in (1, 2):
                        def fn(e, cbi=cbi, vt=vt, col=col, k=k):
                            ins = e.scalar_tensor_tensor(out=scr[vt][:, 0:ncol], in0=scr[cbi][:, k:k + ncol],
                                                         scalar=pc(col + k), in1=scr[vt][:, 0:ncol],
                                                         op0=ALU.mult, op1=ALU.add)
                            return ins, ins
                        pg.op("dve", fn, reads=[scrT[cbi], parT], writes=[scrT[vt]])
                    g = j // 4
                    gi, kk = g % 2, 4 + j % 4
                    for (c0, n, kind) in blocks:
                        def fn(e, a=bg[oi], vt=vt, gi=gi, kk=kk, c0=c0, n=n, kind=kind):
                            ins = e.tensor_tensor(out=grp[gi][:, kk, c0:c0 + n], in0=scr[vt][:, c0:c0 + n],
                                                  in1=acc_ap(a, kind, n), op=ALU.mult)
                            return ins, ins
                        pg.op("dve", fn, reads=[accT[bg[oi]], scrT[vt]], writes=[grpT[gi][kk]])
                return

            def back_conv(p, ub, heads=(0, 1)):
                for oi in heads:
                    j = 2 * p + oi
                    u = ub[oi]
                    y = 8 + oi
                    colw = P_CAW + j * 31

                    def fn(e, u=u, y=y, colw=colw, j=j):
                        ins = e.tensor_scalar(out=scr[y][:, 0:ncol], in0=scr[u][:, 0:ncol], scalar1=pc(colw),
                                              scalar2=pc(P_CAB + j), op0=ALU.mult, op1=ALU.add)
                        return ins, ins
                    pg.op("dve", fn, reads=[scrT[u], parT], writes=[scrT[y]])
                    yb = 10 + oi

                    def fn(e, u=u, yb=yb, colw=colw):
                        ins = e.tensor_scalar(out=scr[yb][:, 0:ncol], in0=scr[u][:, 1:1 + ncol], scalar1=pc(colw + 1),
                                              scalar2=None, op0=ALU.mult)
                        return ins, ins
                    pg.op("dve", fn, reads=[scrT[u], parT], writes=[scrT[yb]])
                    for k in range(2, 31):
                        yy = y if k % 2 == 0 else yb

                        def fn(e, u=u, yy=yy, colw=colw, k=k):
                            ins = e.scalar_tensor_tensor(out=scr[yy][:, 0:ncol], in0=scr[u][:, k:k + ncol],
                                                         scalar=pc(colw + k), in1=scr[yy][:, 0:ncol],
                                                         op0=ALU.mult, op1=ALU.add)
                            return ins, ins
                        pg.op("dve", fn, reads=[scrT[u], parT], writes=[scrT[yy]])

                    def fn(e, y=y, yb=yb):
                        ins = e.tensor_tensor(out=scr[y][:, 0:ncol], in0=scr[y][:, 0:ncol], in1=scr[yb][:, 0:ncol],
                                              op=ALU.add)
                        return ins, ins
                    pg.op("dve", fn, reads=[scrT[yb]], writes=[scrT[y]])

            def back_ln(p):
                for oi in range(2):
                    j = 2 * p + oi
                    y = 8 + oi
                    yc = 10 + oi
                    am = nacc()

                    def fn(e, y=y, am=am):
                        fl = FL()
                        for (c0, n, kind) in blocks:
                            fl(e.matmul(acc_ap(am, kind, n), lhsT=onesF[:, :], rhs=scr[y][:, c0:c0 + n],
                                            start=True, stop=True))
                        return fl.f, fl.l
                    pg.op("pe", fn, reads=[scrT[y], constT], writes=[accT[am]])
                    for (c0, n, kind) in blocks:
                        def fn(e, y=y, yc=yc, am=am, c0=c0, n=n, kind=kind):
                            ins = e.tensor_tensor(out=scr[yc][:, c0:c0 + n], in0=scr[y][:, c0:c0 + n],
                                                  in1=acc_ap(am, kind, n), op=ALU.subtract)
                            return ins, ins
                        pg.op("dve", fn, reads=[accT[am], scrT[y]], writes=[scrT[yc]])

                    def fn(e, y=y, yc=yc):
                        ins = e.activation(out=scr[y][:, 0:ncol], in_=scr[yc][:, 0:ncol], func=AF.Square)
                        return ins, ins
                    pg.op("act", fn, reads=[scrT[yc]], writes=[scrT[y]])
                yield None
                for oi in range(2):
                    j = 2 * p + oi
                    y = 8 + oi
                    yc = 10 + oi
                    av_ = nacc()

                    def fn(e, y=y, av_=av_):
                        fl = FL()
                        for (c0, n, kind) in blocks:
                            fl(e.matmul(acc_ap(av_, kind, n), lhsT=onesF[:, :], rhs=scr[y][:, c0:c0 + n],
                                            start=True, stop=True))
                        return fl.f, fl.l
                    pg.op("pe", fn, reads=[scrT[y], constT], writes=[accT[av_]])
                    for (c0, n, kind) in blocks:
                        def fn(e, y=y, av_=av_, c0=c0, n=n, kind=kind):
                            ins = e.activation(out=scr[y][:, c0:c0 + n], in_=acc_ap(av_, kind, n), func=AF.Sqrt,
                                               bias=pc(P_EPS))
                            return ins, ins
                        pg.op("act", fn, reads=[accT[av_], parT], writes=[scrT[y]])

                        def fnb(e, y=y, c0=c0, n=n):
                            ins = e.reciprocal(out=scr[y][:, c0:c0 + n], in_=scr[y][:, c0:c0 + n])
                            return ins, ins
                        pg.op("dve", fnb, reads=[], writes=[scrT[y]])

                    def fn(e, y=y, yc=yc):
                        ins = e.tensor_tensor(out=scr[yc][:, 0:ncol], in0=scr[yc][:, 0:ncol], in1=scr[y][:, 0:ncol],
                                              op=ALU.mult)
                        return ins, ins
                    pg.op("dve", fn, reads=[scrT[y]], writes=[scrT[yc]])
                    g = j // 4
                    gi, kk = g % 2, j % 4

                    def fn(e, yc=yc, gi=gi, kk=kk, j=j):
                        ins = e.activation(out=grp[gi][:, kk, 0:ncol], in_=scr[yc][:, 0:ncol], func=AF.Silu,
                                           scale=pc(P_LNG + j), bias=pc(P_LNB + j))
                        return ins, ins
                    pg.op("act", fn, reads=[scrT[yc], parT], writes=[grpT[gi][kk]])

            prev = None
            for p in range(9):
                fr = front(p) if p < 8 else None
                ub = None
                if prev is not None and fr is not None:
                    back_conv(prev[0], prev[1], heads=(0,))
                    ub = next(fr)
                    back_conv(prev[0], prev[1], heads=(1,))
                else:
                    if prev is not None:
                        back_conv(prev[0], prev[1])
                    if fr is not None:
                        ub = next(fr)
                if p >= 4 and p % 2 == 0:
                    g = (p - 4) // 2
                    partial_proj("w_out", [(4 * g, 4), (16 + 4 * g, 4)], g % 2, 8, blocks)
                if fr is not None:
                    next(fr)
                bl = back_ln(prev[0]) if prev is not None else None
                if bl is not None:
                    next(bl)
                if fr is not None:
                    next(fr, None)
                if bl is not None:
                    next(bl, None)
                prev = (p, ub) if p < 8 else None
            partial_proj("w_out", [(4 * 3, 4), (16 + 4 * 3, 4)], 3 % 2, 8, blocks)

        if upto >= 2:
            rmsnorm(P_GX, blocks, ncol)
            rhs_xn = lambda k, c0, n: xn[:, k, c0:c0 + n]
            tchunks = []
            if ti == 0:
                tchunks.append((0, HALO, "h", 0))
            for tc in range(4):
                tchunks.append((mainc0 + tc * 128, 128, "m", tc * 128))

            def do_q(hd):
                gi = 0
                for cbk in range(4):
                    accs = fullk("w_q", hd * 1024 + cbk * 256, blocks, ncol, rhs_xn, xnT)
                    for oi in range(2):
                        kk = cbk * 2 + oi
                        for (c0, n, kind) in blocks:
                            def fn(e, a=accs[oi], kk=kk, c0=c0, n=n, kind=kind):
                                ins = e.activation(out=grp[gi][:, kk, c0:c0 + n], in_=acc_ap(a, kind, n), func=AF.Copy)
                                return ins, ins
                            pg.op("act", fn, reads=[accT[accs[oi]]], writes=[grpT[gi][kk]])

            def do_s(hd):
                gi = 0

                def fk(e, hd=hd):
                    ins = e.dma_start(out=ktb[:, :, :], in_=kt_scr[hd].rearrange("p (k m) -> p k m", m=MEM))
                    return ins, ins
                pg.dma("sp", fk, ktT, reads=(ktscrT[hd],), writes=(ktT,))
                apt = nacc()
                pend_pt = None
                first_pt = True
                for ci, (c0, n, kind, pcol) in enumerate(tchunks):
                    s_list = [a_ for a_ in range(cur["nacc"]) if a_ != apt]
                    asb = s_list[ci % len(s_list)]
                    soff = 0

                    def fs(e, c0=c0, n=n, asb=asb, soff=soff):
                        fl = FL()
                        for c in range(8):
                            fl(e.matmul(ps[0:n, asb, soff:soff + MEM], lhsT=grp[gi][:, c, c0:c0 + n],
                                            rhs=ktb[:, c, :], start=(c == 0), stop=(c == 7)))
                        return fl.f, fl.l
                    pg.op("pe", fs, reads=grpT[gi] + [ktT], writes=[accT[asb]])
                    sm = nxt("sm", 4)
                    pt = nxt("pt", 2)

                    def f1(e, n=n, asb=asb, soff=soff, sm=sm):
                        ins = e.reduce_max(out=small[sm][0:n, 0:1], in_=ps[0:n, asb, soff:soff + MEM], axis=AX.X)
                        return ins, ins
                    pg.op("dve", f1, reads=[accT[asb]], writes=[smallT[sm]])

                    def f2(e, n=n, sm=sm):
                        ins = e.tensor_scalar(out=small[sm][0:n, 1:2], in0=small[sm][0:n, 0:1], scalar1=-1.0 / 32.0,
                                              scalar2=None, op0=ALU.mult)
                        return ins, ins
                    pg.op("dve", f2, reads=[], writes=[smallT[sm]])

                    def f3(e, n=n, asb=asb, soff=soff, sm=sm, pt=pt):
                        ins = e.activation(out=Pt[pt][0:n, :], in_=ps[0:n, asb, soff:soff + MEM], func=AF.Exp,
                                           bias=small[sm][0:n, 1:2], scale=1.0 / 32.0, accum_out=small[sm][0:n, 2:3])
                        return ins, ins
                    pg.op("act", f3, reads=[accT[asb], smallT[sm]], writes=[PtT[pt], smallT[sm]])

                    def f4(e, n=n, sm=sm):
                        ins = e.reciprocal(out=small[sm][0:n, 3:4], in_=small[sm][0:n, 2:3])
                        return ins, ins
                    pg.op("dve", f4, reads=[], writes=[smallT[sm]])

                    def f5(e, n=n, sm=sm, pt=pt):
                        ins = e.tensor_scalar(out=Pt[pt][0:n, :], in0=Pt[pt][0:n, :], scalar1=small[sm][0:n, 3:4],
                                              scalar2=None, op0=ALU.mult)
                        return ins, ins
                    pg.op("dve", f5, reads=[smallT[sm]], writes=[PtT[pt]])

                    def f6(e, n=n, kind=kind, pcol=pcol, pt=pt, apt=apt):
                        fl = FL()
                        for mc in range(2):
                            if kind == "h":
                                dst = ps[:, 4 + apt, 0:32].bitcast(BF16)[:, mc * 32:mc * 32 + n]
                            else:
                                dst = ps[:, apt, :].bitcast(BF16)[:, mc * 512 + pcol:mc * 512 + pcol + n]
                            fl(e.transpose(dst, Pt[pt][0:n, mc * 128:(mc + 1) * 128], identB[0:n, 0:n]))
                        return fl.f, fl.l
                    if pend_pt is not None:
                        pg.op("pe", pend_pt[0], reads=[PtT[pend_pt[1]], constT], writes=[accT[apt]])
                    pend_pt = (f6, pt)
                pg.op("pe", pend_pt[0], reads=[PtT[pend_pt[1]], constT], writes=[accT[apt]])
                for (c0, n, kind) in blocks:
                    def f7(e, c0=c0, n=n, kind=kind, apt=apt):
                        if kind == "h":
                            src = ps[:, 4 + apt, 0:32].bitcast(BF16).rearrange("p (m t) -> p m t", t=32)
                        else:
                            src = ps[:, apt, :].bitcast(BF16).rearrange("p (m t) -> p m t", t=512)
                        ins = e.activation(out=PTsb[:, :, c0:c0 + n], in_=src, func=AF.Copy)
                        return ins, ins
                    pg.op("act", f7, reads=[accT[apt]], writes=[PTsbT])

            def do_pv(hd):
                gi = 1

                def fv(e, hd=hd):
                    ins = e.dma_start(out=vb[:, :, :], in_=v_scr[hd].rearrange("p (k m) -> p k m", m=1024))
                    return ins, ins
                pg.dma("sp", fv, vT, reads=(vscrT[hd],), writes=(vT,))
                for c in range(8):
                    a = nacc()

                    def fp(e, c=c, a=a):
                        fl = FL()
                        for mc in range(2):
                            for (c0, n, kind) in blocks:
                                fl(e.matmul(acc_ap(a, kind, n), lhsT=vb[:, mc, c * 128:(c + 1) * 128],
                                                rhs=PTsb[:, mc, c0:c0 + n], start=(mc == 0), stop=(mc == 1)))
                        return fl.f, fl.l
                    pg.op("pe", fp, reads=[vT, PTsbT], writes=[accT[a]])
                    for (c0, n, kind) in blocks:
                        def fe(e, c=c, a=a, c0=c0, n=n, kind=kind):
                            ins = e.activation(out=grp[gi][:, c, c0:c0 + n], in_=acc_ap(a, kind, n), func=AF.Copy)
                            return ins, ins
                        pg.op("act", fe, reads=[accT[a]], writes=[grpT[gi][c]])
                partial_proj("w_o", [(hd * 8, 8)], gi, 8, blocks)

            do_q(0)
            do_s(0)
            for hd in range(4):
                if hd + 1 < 4:
                    do_q(hd + 1)
                do_pv(hd)
                if hd + 1 < 4:
                    do_s(hd + 1)

        if upto >= 3:
            rmsnorm(P_GF, blocks, ncol)
            rhs_xn = lambda k, c0, n: xn[:, k, c0:c0 + n]
            mblk = [blocks[-1]]
            for fp_ in range(FC // 2):
                gacc = fullk("w_gate", fp_ * 256, blocks, ncol, rhs_xn, xnT)
                uacc = fullk("w_up", fp_ * 256, mblk, ncol, rhs_xn, xnT)
                for oi in range(2):
                    fc = fp_ * 2 + oi
                    fg = fc // 8
                    gi, kk = fg % 2, fc % 8
                    gb = fc % 3
                    gc = 3 + fc % 3

                    def fh(e, fc=fc, gb=gb):
                        ins = e.activation(out=scr[gb][:, 0:2], in_=ghist[:, fc, :], func=AF.Copy)
                        return ins, ins
                    pg.op("act", fh, reads=[ghT[fc]], writes=[scrT[gb]])
                    for (c0, n, kind) in blocks:
                        def fn(e, a=gacc[oi], gb=gb, c0=c0, n=n, kind=kind):
                            if kind == "h":
                                ins = e.activation(out=scr[gb][:, 2 + c0:2 + c0 + n], in_=acc_ap(a, kind, n),
                                                   func=AF.Identity, scale=pc(P_MASK))
                            else:
                                ins = e.activation(out=scr[gb][:, 2 + c0:2 + c0 + n], in_=acc_ap(a, kind, n),
                                                   func=AF.Copy)
                            return ins, ins
                        pg.op("act", fn, reads=[accT[gacc[oi]], parT], writes=[scrT[gb]])

                    def fh2(e, fc=fc, gb=gb):
                        ins = e.activation(out=ghist[:, fc, :], in_=scr[gb][:, ncol:ncol + 2], func=AF.Copy)
                        return ins, ins
                    pg.op("act", fh2, reads=[scrT[gb]], writes=[ghT[fc]])
                    col = P_CFW + fc * 3

                    def fn(e, gb=gb, gc=gc, col=col):
                        ins = e.tensor_scalar(out=scr[gc][:, 0:ncol], in0=scr[gb][:, 0:ncol], scalar1=pc(col),
                                              scalar2=None, op0=ALU.mult)
                        return ins, ins
                    pg.op("dve", fn, reads=[scrT[gb], parT], writes=[scrT[gc]])
                    for k in (1, 2):
                        def fn(e, gb=gb, gc=gc, col=col, k=k):
                            ins = e.scalar_tensor_tensor(out=scr[gc][:, 0:ncol], in0=scr[gb][:, k:k + ncol],
                                                         scalar=pc(col + k), in1=scr[gc][:, 0:ncol],
                                                         op0=ALU.mult, op1=ALU.add)
                            return ins, ins
                        pg.op("dve", fn, reads=[scrT[gb], parT], writes=[scrT[gc]])

                    def fn(e, gb=gb, gc=gc):
                        ins = e.activation(out=scr[gb][:, 0:ncol], in_=scr[gc][:, 0:ncol], func=AF.Silu)
                        return ins, ins
                    pg.op("act", fn, reads=[scrT[gc]], writes=[scrT[gb]])
                    for (c0, n, kind) in mblk:
                        def fn(e, a=uacc[oi], gb=gb, gi=gi, kk=kk, c0=c0, n=n, kind=kind):
                            ins = e.tensor_tensor(out=grp[gi][:, kk, c0:c0 + n], in0=scr[gb][:, c0:c0 + n],
                                                  in1=acc_ap(a, kind, n), op=ALU.mult)
                            return ins, ins
                        pg.op("dve", fn, reads=[accT[uacc[oi]], scrT[gb]], writes=[grpT[gi][kk]])
                    if kk == 7 or fc == FC - 1:
                        nk = kk + 1
                        partial_proj("w_down", [(fg * 8, nk)], gi, nk, mblk)

        if upto >= 4:
            rmsnorm(P_GFIN, blocks, ncol, final=True)
        nxt_tile = ti + 1 < ntiles
        a_next = None
        if nxt_tile and upto >= 1:
            a_next = nacc()
            cur["hold"].add(a_next)
        nblocks = [(0, NTM, "m")]
        for db in range(8):
            for tc in range(4):
                col0 = mainc0 + tc * 128
                orow = ti * NTM + tc * 128
                a = nacc()

                def ft(e, db=db, a=a, col0=col0):
                    fl = FL()
                    for i in range(4):
                        fl(e.transpose(ps[:, a, i * 128:(i + 1) * 128], h[:, db * 4 + i, col0:col0 + 128],
                                       par[:, P_ID:P_ID + 128]))
                    return fl.f, fl.l
                pg.op("pe", ft, reads=hT[db * 4:(db + 1) * 4] + [constT], writes=[accT[a]])
                si = nxt("stg", NSTG + 1)

                def fe(e, a=a, si=si):
                    ins = e.activation(out=stg[si][:, 0:512], in_=ps[:, a, :], func=AF.Copy)
                    return ins, ins
                pg.op("act", fe, reads=[accT[a]], writes=[stgT[si]])

                def fo(e, si=si, orow=orow, db=db):
                    ins = e.dma_start(out=out_d[orow:orow + 128, db * 512:(db + 1) * 512], in_=stg[si][:, 0:512])
                    return ins, ins
                pg.dma("sp", fo, stgT[si], reads=(stgT[si],), writes=())
            if nxt_tile:
                nrow0 = HALO + (ti + 1) * NTM
                for tc in range(4):
                    r = nrow0 + tc * 128
                    load_unit(xin[r:r + 128, :], 128, tc * 128, db)
                if a_next is not None:
                    for c in range(db * 4, db * 4 + 4):
                        rms_stat_chunk(a_next, c, nblocks, NTM)
        if a_next is not None:
            cur["pre_acc"] = a_next

    for ti_ in range(ntiles):
        do_tile(ti_)

    pg.wait_all("sp", stgT)
    pg.emit(nc, es)
    print('sbuf bytes remaining', nc.sbuf_bytes_remaining)
    es.close()
    return nc


def pack_params(inp, half):
    P = np.zeros((128, NPAR), np.float32)

    def pm(v):
        return np.ascontiguousarray(v.reshape(-1, 128).T)
    P[:, P_GMIX:P_GMIX + 32] = pm(inp["g_mix"][0])
    P[:, P_GX:P_GX + 32] = pm(inp["g_xattn"][0])
    P[:, P_GF:P_GF + 32] = pm(inp["g_ffn"][0])
    P[:, P_GFIN:P_GFIN + 32] = pm(inp["g_final"])
    P[:, P_GMEM:P_GMEM + 32] = pm(inp["g_mem"])
    caw = inp["conv_a_w"][0]
    P[:, P_CAW:P_CAW + 496] = caw.reshape(31, 16, 128).transpose(2, 1, 0).reshape(128, 496)
    P[:, P_CAB:P_CAB + 16] = pm(inp["conv_a_b"][0])
    P[:, P_LNG:P_LNG + 16] = pm(inp["ln_a_g"][0])
    P[:, P_LNB:P_LNB + 16] = pm(inp["ln_a_b"][0])
    cbw = inp["conv_b_w"][0]
    P[:, P_CBW:P_CBW + 48] = cbw.reshape(3, 16, 128).transpose(2, 1, 0).reshape(128, 48)
    cfw = inp["conv_f_w"][0]
    P[:, P_CFW:P_CFW + FC * 3] = cfw.reshape(3, FC, 128).transpose(2, 1, 0).reshape(128, FC * 3)
    P[:, P_MASK] = float(half)
    P[:, P_EPS] = EPS
    P[:, P_ID:P_ID + 128] = np.eye(128, dtype=np.float32)
    return P


_NC_CACHE = {}


def make_in_maps(inp, ncores=8):
    x = np.asarray(inp["x"], np.float32)
    mem = np.asarray(inp["mem"], np.float32)
    wts = {k: np.ascontiguousarray(np.asarray(inp[k], np.float32)[0]) for k in
           ("w_in", "w_out", "w_q", "w_k", "w_v", "w_o", "w_gate", "w_up", "w_down")}
    inp = {k: np.asarray(v, np.float32) for k, v in inp.items() if k not in ("x", "mem") and k not in wts}
    maps = []
    for c in range(ncores):
        b, half = c // 2, c % 2
        t0 = half * TOK_CORE
        xi = np.zeros((HALO + TOK_CORE, D), np.float32)
        if half == 0:
            xi[HALO:] = x[b, 0:TOK_CORE]
        else:
            xi[:] = x[b, t0 - HALO:t0 + TOK_CORE]
        m = {"xin": xi, "memin": np.ascontiguousarray(mem[b]), "params": pack_params(inp, half)}
        m.update(wts)
        maps.append(m)
    return maps


def kernel(**inputs):
    key = "full"
    if key not in _NC_CACHE:
        _NC_CACHE[key] = build()
    nc = _NC_CACHE[key]
    maps = make_in_maps(inputs, 8)
    res = run_bass_kernel_spmd(nc, maps, core_ids=list(range(8)))
    out = np.zeros((4, SEQ, D), np.float32)
    for c in range(8):
        b, half = c // 2, c % 2
        out[b, half * TOK_CORE:(half + 1) * TOK_CORE] = res.results[c]["out"]
    return out
```

```python
import numpy as np
from contextlib import ExitStack
import concourse.bass as bass
import concourse.mybir as mybir
from concourse.bass_utils import run_bass_kernel_spmd

F32 = mybir.dt.float32
BF16 = mybir.dt.bfloat16
ALU = mybir.AluOpType
AF = mybir.ActivationFunctionType
AX = mybir.AxisListType

D = 4096
DC = 32
SEQ = 4096
TOK_CORE = 2048
HALO = 32
NTM = 512
NTC = NTM + HALO
MEM = 256
INW = 10240
DFF = 11008
FC = 86
EPS = 1e-6
NW = 4
NSCR = 12
NSTG = 3
SCRW = NTC + 32

P_GMIX, P_GX, P_GF, P_GFIN, P_GMEM = 0, 32, 64, 96, 128
P_CAW = 160
P_CAB = P_CAW + 16 * 31
P_LNG = P_CAB + 16
P_LNB = P_LNG + 16
P_CBW = P_LNB + 16
P_CFW = P_CBW + 48
P_MASK = P_CFW + FC * 3
P_EPS = P_MASK + 1
P_ID = P_MASK + 2
NPAR = P_ID + 128


class T:
    __slots__ = ("name", "w", "r", "dsem", "dcount")

    def __init__(self, name):
        self.name = name
        self.w = None
        self.r = {}
        self.dsem = None
        self.dcount = 0


class Eng:
    def __init__(self, name):
        self.name = name
        self.count = 0
        self.ops = []
        self.waited = {}
        self.pending = False


class Prog:
    def __init__(self):
        self.engs = {n: Eng(n) for n in ("pe", "act", "dve", "pool", "sp")}
        self.dsems = []

    def _deps(self, eng, reads, writes):
        need = {}
        def add(ev):
            if ev is None:
                return
            sk, v = ev
            if eng.waited.get(sk, 0) < v and need.get(sk, 0) < v:
                need[sk] = v
        for t in reads:
            add(t.w)
        for t in writes:
            add(t.w)
            for sk, v in t.r.items():
                add((sk, v))
        for sk, v in need.items():
            eng.waited[sk] = v
        return list(need.items())

    def op(self, en, fn, reads=(), writes=(), signal=True, sepwait=False):
        eng = self.engs[en]
        waits = self._deps(eng, reads, writes)
        ev = (en, eng.count + 1)
        if signal:
            eng.count += 1
            eng.pending = False
        else:
            eng.pending = True
        eng.ops.append(("op", fn, waits, signal, sepwait))
        for t in reads:
            if t.r.get(ev[0], 0) < ev[1]:
                t.r[ev[0]] = ev[1]
        for t in writes:
            t.w = ev
            t.r = {}
        return ev

    def dma(self, en, fn, semT, reads=(), writes=()):
        eng = self.engs[en]
        waits = self._deps(eng, reads, writes)
        if semT.dsem is None:
            semT.dsem = "d%d" % len(self.dsems)
            self.dsems.append(semT.dsem)
        semT.dcount += 1
        ev = (semT.dsem, 16 * semT.dcount)
        eng.ops.append(("dma", fn, waits, semT.dsem, True))
        for t in reads:
            if t.r.get(ev[0], 0) < ev[1]:
                t.r[ev[0]] = ev[1]
        for t in writes:
            t.w = ev
            t.r = {}
        return ev

    def wait_all(self, en, tiles):
        eng = self.engs[en]
        waits = self._deps(eng, [], tiles)
        eng.ops.append(("wait", None, waits, None, True))

    def emit(self, nc, es):
        sems = {}
        for n in list(self.engs) + self.dsems:
            sems[n] = es.enter_context(nc.semaphore("s_" + n))
        block = es.enter_context(nc.Block())
        hw = {"pe": block.tensor, "act": block.scalar, "dve": block.vector,
              "pool": block.gpsimd, "sp": block.sync}

        def make(en):
            eng = self.engs[en]

            def body(e):
                for kind, fn, waits, sig, sepwait in eng.ops:
                    if kind == "wait":
                        for sk, v in waits:
                            e.wait_ge(sems[sk], v)
                        continue
                    if sepwait or len(waits) == 0:
                        for sk, v in waits:
                            e.wait_ge(sems[sk], v)
                        first, last = fn(e)
                    else:
                        for sk, v in waits[1:]:
                            e.wait_ge(sems[sk], v)
                        first, last = fn(e)
                        first._wait_ge(sems[waits[0][0]], waits[0][1])
                    if kind == "dma":
                        last.then_inc(sems[sig], 16)
                    elif sig:
                        last.then_inc(sems[en], 1)
            return body
        for en in self.engs:
            if self.engs[en].ops:
                hw[en](make(en))


class FL:
    def __init__(self):
        self.f = None
        self.l = None

    def __call__(self, ins):
        if self.f is None:
            self.f = ins
        self.l = ins
        return ins


def build(ntiles=4, upto=4, dbg=False):
    nc = bass.Bass("TRN2", target_bir_lowering=False)
    xin = nc.dram_tensor("xin", [HALO + TOK_CORE, D], F32, kind="ExternalInput").ap()
    memin = nc.dram_tensor("memin", [MEM, D], F32, kind="ExternalInput").ap()
    par_d = nc.dram_tensor("params", [128, NPAR], F32, kind="ExternalInput").ap()
    wd = {}
    for name, shp in (("w_in", [D, INW]), ("w_out", [D, D]), ("w_q", [D, D]), ("w_k", [D, D]),
                      ("w_v", [D, D]), ("w_o", [D, D]), ("w_gate", [D, DFF]), ("w_up", [D, DFF]),
                      ("w_down", [DFF, D])):
        wd[name] = nc.dram_tensor(name, shp, F32, kind="ExternalInput").ap().rearrange("(k p) c -> p k c", p=128)
    out_d = nc.dram_tensor("out", [TOK_CORE, D], F32, kind="ExternalOutput").ap()
    kt_scr = nc.dram_tensor("kt_scr", [4, 128, 8 * MEM], BF16, kind="Internal").ap()
    v_scr = nc.dram_tensor("v_scr", [4, 128, 2 * 1024], BF16, kind="Internal").ap()

    pg = Prog()
    es = ExitStack()
    es.enter_context(nc.allow_low_precision("bf16 matmul operands with fp32 PSUM accumulation"))

    def sb(name, shape, dt):
        return es.enter_context(nc.sbuf_tensor(name, shape, dt))

    h = sb("h", [128, DC, NTC], F32)
    xn = sb("xn", [128, DC, NTC], BF16)
    grp = [sb("grp%d" % i, [128, 8, NTC], BF16) for i in range(2)]
    wsl = [sb("wsl%d" % i, [128, 4096], BF16) for i in range(NW)]
    ktb = sb("ktb", [128, 8, MEM], BF16)
    vb = sb("vb", [128, 2, 1024], BF16)
    par = sb("par", [128, NPAR], F32)
    identB = sb("identB", [128, 128], BF16)
    onesB = sb("onesB", [128, 128], BF16)
    onesF = sb("onesF", [128, 128], F32)
    sqbf = [sb("sqbf%d" % i, [128, NTC], BF16) for i in range(2)]
    scr = [sb("scr%d" % i, [128, SCRW], F32) for i in range(NSCR)]
    rstd = sb("rstd", [128, NTC], F32)
    uhist = sb("uhist", [128, 16, 30], F32)
    cbhist = sb("cbhist", [128, 16, 2], F32)
    ghist = sb("ghist", [128, FC, 2], F32)
    stg = [sb("stg%d" % i, [128, 512], F32) for i in range(NSTG)]
    Pt = [sb("Pt%d" % i, [128, MEM], BF16) for i in range(2)]
    PTsb = sb("PTsb", [128, 2, NTC], BF16)
    small = [sb("small%d" % i, [128, 4], F32) for i in range(4)]
    ps = es.enter_context(nc.psum_tensor("ps", [128, 8, 512], F32))

    hT = [T("h%d" % c) for c in range(DC)]
    xnT = [T("xn%d" % c) for c in range(DC)]
    grpT = [[T("g%d_%d" % (i, k)) for k in range(8)] for i in range(2)]
    wT = [T("w%d" % i) for i in range(NW)]
    ktT, vT, parT = T("kt"), T("v"), T("par")
    constT = T("const")
    sqT = [T("sq%d" % i) for i in range(2)]
    scrT = [T("scr%d" % i) for i in range(NSCR)]
    rstdT = T("rstd")
    uhT = [T("uh%d" % j) for j in range(16)]
    cbhT = [T("cbh%d" % j) for j in range(16)]
    ghT = [T("gh%d" % j) for j in range(FC)]
    stgT = [T("stg%d" % i) for i in range(NSTG)]
    stg = stg + [rstd]
    stgT = stgT + [rstdT]
    PtT = [T("Pt%d" % i) for i in range(2)]
    PTsbT = T("PTsb")
    smallT = [T("sm%d" % i) for i in range(4)]
    accT = [T("acc%d" % i) for i in range(8)]
    ktscrT = [T("ktscr%d" % i) for i in range(4)]
    vscrT = [T("vscr%d" % i) for i in range(4)]

    cnt = {"acc": 0, "w": 0, "stg": 0, "sq": 0, "pt": 0, "sm": 0}

    def nxt(key, n):
        i = cnt[key] % n
        cnt[key] += 1
        return i

    cur = {"nacc": 4, "hold": set(), "pre_acc": None}

    def nacc():
        while True:
            i = nxt("acc", cur["nacc"])
            if i not in cur["hold"]:
                return i

    def acc_ap(ai, kind, n, c0=0):
        if kind == "h":
            return ps[:, 4 + ai, c0:c0 + n]
        return ps[:, ai, c0:c0 + n]

    def pc(col):
        return par[:, col:col + 1]

    def wload(view_parts):
        si = nxt("w", NW)
        for (lo, hi, src, shape3) in view_parts:
            def fn(e, lo=lo, hi=hi, src=src, shape3=shape3, si=si):
                dst = wsl[si][:, lo:hi].rearrange("p (k c) -> p k c", c=shape3)
                i = e.dma_start(out=dst, in_=src)
                return i, i
            pg.dma("pool", fn, wT[si], reads=(), writes=(wT[si],))
        return si

    def fullk(wname, col0, blocks, ncols, rhs_chunk, rhs_tiles, n_acc=2):
        accs = [nacc() for _ in range(n_acc)]
        for kh in range(2):
            si = wload([(0, 4096, wd[wname][:, kh * 16:(kh + 1) * 16, col0:col0 + 256], 256)])

            for oi in range(n_acc):
                def fn(e, kh=kh, si=si, accs=accs, oi=oi):
                    fl = FL()
                    wv = wsl[si][:, :].rearrange("p (k c) -> p k c", c=256)
                    for kk in range(16):
                        k = kh * 16 + kk
                        for (c0, n, kind) in blocks:
                            fl(e.matmul(acc_ap(accs[oi], kind, n), lhsT=wv[:, kk, oi * 128:(oi + 1) * 128],
                                        rhs=rhs_chunk(k, c0, n), start=(k == 0), stop=(k == 31)))
                    return fl.f, fl.l
                pg.op("pe", fn, reads=[wT[si]] + rhs_tiles[kh * 16:(kh + 1) * 16], writes=[accT[accs[oi]]])
        return accs

    def partial_proj(wname, rowsets, gi, nk, blocks, obs=range(8)):
        for ob in obs:
            parts = []
            lo = 0
            for (r0, rn) in rowsets:
                parts.append((lo, lo + rn * 512, wd[wname][:, r0:r0 + rn, ob * 512:(ob + 1) * 512], 512))
                lo += rn * 512
            si = wload(parts)
            accs = [nacc() for _ in range(4)]

            for i in range(4):
                def fn(e, si=si, accs=accs, i=i):
                    fl = FL()
                    wv = wsl[si][:, 0:nk * 512].rearrange("p (k c) -> p k c", c=512)
                    for kk in range(nk):
                        for (c0, n, kind) in blocks:
                            fl(e.matmul(acc_ap(accs[i], kind, n), lhsT=wv[:, kk, i * 128:(i + 1) * 128],
                                        rhs=grp[gi][:, kk, c0:c0 + n], start=(kk == 0), stop=(kk == nk - 1)))
                    return fl.f, fl.l
                pg.op("pe", fn, reads=[wT[si]] + grpT[gi][:nk], writes=[accT[accs[i]]])
            for i in range(4):
                oc = ob * 4 + i
                for (c0, n, kind) in blocks:
                    def fn2(e, oc=oc, c0=c0, n=n, kind=kind, a=accs[i]):
                        ins = e.tensor_tensor(out=h[:, oc, c0:c0 + n], in0=h[:, oc, c0:c0 + n],
                                              in1=acc_ap(a, kind, n), op=ALU.add)
                        return ins, ins
                    pg.op("dve", fn2, reads=[accT[accs[i]]], writes=[hT[oc]])

    def rms_stat_chunk(a, c, blocks, ncol):
        qi = nxt("sq", 2)

        def fn(e, c=c, qi=qi):
            ins = e.activation(out=sqbf[qi][:, 0:ncol], in_=h[:, c, 0:ncol], func=AF.Square)
            return ins, ins
        pg.op("act", fn, reads=[hT[c]], writes=[sqT[qi]])

        def fn2(e, c=c, qi=qi, a=a):
            fl = FL()
            for (c0, n, kind) in blocks:
                fl(e.matmul(acc_ap(a, kind, n), lhsT=onesB[:, :], rhs=sqbf[qi][:, c0:c0 + n],
                            start=(c == 0), stop=(c == DC - 1)))
            return fl.f, fl.l
        pg.op("pe", fn2, reads=[sqT[qi], constT], writes=[accT[a]])

    def rmsnorm(gcol, blocks, ncol, final=False, pre_acc=None):
        if pre_acc is None:
            a = nacc()
            for c in range(DC):
                rms_stat_chunk(a, c, blocks, ncol)
        else:
            a = pre_acc
        for (c0, n, kind) in blocks:
            def fn3(e, c0=c0, n=n, kind=kind, a=a):
                ins = e.activation(out=rstd[:, c0:c0 + n], in_=acc_ap(a, kind, n), func=AF.Sqrt, bias=pc(P_EPS))
                return ins, ins
            pg.op("act", fn3, reads=[accT[a], parT], writes=[rstdT])

            def fn3b(e, c0=c0, n=n):
                ins = e.reciprocal(out=rstd[:, c0:c0 + n], in_=rstd[:, c0:c0 + n])
                return ins, ins
            pg.op("dve", fn3b, reads=[], writes=[rstdT])
        cur["hold"].discard(a)
        for c in range(DC):
            def fn4(e, c=c):
                dst = h[:, c, 0:ncol] if final else xn[:, c, 0:ncol]
                ins = e.scalar_tensor_tensor(out=dst, in0=h[:, c, 0:ncol], scalar=pc(gcol + c),
                                             in1=rstd[:, 0:ncol], op0=ALU.mult, op1=ALU.mult)
                return ins, ins
            pg.op("dve", fn4, reads=[hT[c], rstdT, parT], writes=[hT[c] if final else xnT[c]])

    def load_rows(src_rows_ap, nrows, dst_cols, dstT_list, dst_is_h=True):
        for db in range(8):
            load_unit(src_rows_ap, nrows, dst_cols, db)

    def load_unit(src_rows_ap, nrows, dst_cols, db):
        if True:
            si = nxt("stg", NSTG + 1)

            def fn(e, db=db, si=si):
                ins = e.dma_start(out=stg[si][0:nrows, 0:512], in_=src_rows_ap[:, db * 512:(db + 1) * 512])
                return ins, ins
            pg.dma("sp", fn, stgT[si], reads=(), writes=(stgT[si],))
            a = nacc()

            def fn2(e, si=si, a=a):
                fl = FL()
                for i in range(4):
                    fl(e.transpose(ps[:, a, i * nrows:(i + 1) * nrows], stg[si][0:nrows, i * 128:(i + 1) * 128],
                                       par[0:nrows, P_ID:P_ID + nrows]))
                return fl.f, fl.l
            pg.op("pe", fn2, reads=[stgT[si], constT], writes=[accT[a]])

            def fn3(e, db=db, a=a):
                src = ps[:, a, 0:4 * nrows].rearrange("p (i t) -> p i t", t=nrows)
                ins = e.activation(out=h[:, db * 4:(db + 1) * 4, dst_cols:dst_cols + nrows], in_=src, func=AF.Copy)
                return ins, ins
            pg.op("act", fn3, reads=[accT[a]], writes=hT[db * 4:(db + 1) * 4])

    def f_par(e):
        ins = e.dma_start(out=par[:, :], in_=par_d)
        return ins, ins
    pg.dma("sp", f_par, parT, writes=(parT,))

    def f_init(e):
        e.memset(onesB[:, :], 1.0 / 4096.0)
        e.memset(onesF[:, :], 1.0 / 128.0)
        e.memset(uhist[:, :, :], 0.0)
        e.memset(cbhist[:, :, :], 0.0)
        ins = e.memset(ghist[:, :, :], 0.0)
        return ins, ins
    pg.op("dve", f_init, writes=[constT] + uhT + cbhT + ghT)

    def f_ident(e):
        ins = e.activation(out=identB[:, :], in_=par[:, P_ID:P_ID + 128], func=AF.Copy)
        return ins, ins
    pg.op("act", f_ident, reads=[parT], writes=[constT])

    mblocks = [(0, MEM, "m")]
    if upto >= 2:
        for mc in range(2):
            load_rows(memin[mc * 128:(mc + 1) * 128, :], 128, mc * 128, hT)
        rmsnorm(P_GMEM, mblocks, MEM)
        for cb in range(16):
            accs = fullk("w_k", cb * 256, mblocks, MEM, lambda k, c0, n: xn[:, k, c0:c0 + n], xnT)
            for oi in range(2):
                oc = cb * 2 + oi
                gi, kk = (oc // 8) % 2, oc % 8

                def fn(e, a=accs[oi], gi=gi, kk=kk):
                    ins = e.activation(out=grp[gi][:, kk, 0:MEM], in_=ps[:, a, 0:MEM], func=AF.Copy)
                    return ins, ins
                pg.op("act", fn, reads=[accT[accs[oi]]], writes=[grpT[gi][kk]])
            if cb % 4 == 3:
                hd = cb // 4
                gi = hd % 2

                def fn(e, hd=hd, gi=gi):
                    ins = e.dma_start(out=kt_scr[hd].rearrange("p (k m) -> p k m", m=MEM), in_=grp[gi][:, :, 0:MEM])
                    return ins, ins
                pg.dma("sp", fn, ktscrT[hd], reads=grpT[gi], writes=(ktscrT[hd],))
        for cb in range(16):
            accs = [nacc() for _ in range(2)]
            for kh in range(2):
                si = wload([(0, 4096, wd["w_v"][:, kh * 16:(kh + 1) * 16, cb * 256:(cb + 1) * 256], 256)])

                def fn(e, kh=kh, si=si, accs=accs):
                    fl = FL()
                    wv = wsl[si][:, :].rearrange("p (k c) -> p k c", c=256)
                    for mc in range(2):
                        for kk in range(16):
                            k = kh * 16 + kk
                            fl(e.matmul(ps[:, accs[mc], 0:256], lhsT=xn[:, k, mc * 128:(mc + 1) * 128],
                                            rhs=wv[:, kk, :], start=(k == 0), stop=(k == 31)))
                    return fl.f, fl.l
                pg.op("pe", fn, reads=[wT[si]] + xnT[kh * 16:(kh + 1) * 16], writes=[accT[a] for a in accs])
            hd = cb // 4
            gi = hd % 2
            for mc in range(2):
                def fn(e, a=accs[mc], gi=gi, mc=mc, cb=cb):
                    flat = grp[gi][:, :, :].rearrange("p k c -> p (k c)")
                    off = mc * 1024 + (cb % 4) * 256
                    ins = e.activation(out=flat[:, off:off + 256], in_=ps[:, a, 0:256], func=AF.Copy)
                    return ins, ins
                pg.op("act", fn, reads=[accT[accs[mc]]], writes=grpT[gi])
            if cb % 4 == 3:
                def fn(e, hd=hd, gi=gi):
                    flat = grp[gi][:, :, :].rearrange("p k c -> p (k c)")
                    ins = e.dma_start(out=v_scr[hd], in_=flat[:, 0:2048])
                    return ins, ins
                pg.dma("sp", fn, vscrT[hd], reads=grpT[gi], writes=(vscrT[hd],))

    def do_tile(ti):
        if ti == 0:
            blocks = [(0, HALO, "h"), (HALO, NTM, "m")]
            ncol = NTC
            row0 = 0
        else:
            blocks = [(0, NTM, "m")]
            ncol = NTM
            row0 = HALO + ti * NTM
            if ti == 1:
                for i in range(4):
                    accT[4 + i].w = accT[i].w
                    accT[4 + i].r = dict(accT[i].r)
                cur["nacc"] = 8
                cnt["acc"] = 0
        mainc0 = blocks[-1][0]

        pre_acc = cur["pre_acc"]
        cur["pre_acc"] = None
        if ti == 0:
            load_rows(xin[0:HALO, :], HALO, 0, hT)
            for tc in range(4):
                load_rows(xin[HALO + tc * 128:HALO + (tc + 1) * 128, :], 128, mainc0 + tc * 128, hT)

        if upto >= 1:
            rmsnorm(P_GMIX, blocks, ncol, pre_acc=pre_acc)
            rhs_xn = lambda k, c0, n: xn[:, k, c0:c0 + n]
            pend_back = []
            pend_proj = []

            def front(p):
                av = fullk("w_in", p * 256, blocks, ncol, rhs_xn, xnT)
                ag = fullk("w_in", 2048 + p * 256, blocks, ncol, rhs_xn, xnT)
                ub = []
                for oi in range(2):
                    j = 2 * p + oi
                    sg = oi
                    u = 2 + (2 * p + oi) % 4
                    ub.append(u)
                    for (c0, n, kind) in blocks:
                        def fn(e, a=ag[oi], sg=sg, c0=c0, n=n, kind=kind):
                            ins = e.activation(out=scr[sg][:, c0:c0 + n], in_=acc_ap(a, kind, n), func=AF.Sigmoid)
                            return ins, ins
                        pg.op("act", fn, reads=[accT[ag[oi]]], writes=[scrT[sg]])

                    def fnh(e, j=j, u=u):
                        ins = e.activation(out=scr[u][:, 0:30], in_=uhist[:, j, :], func=AF.Copy)
                        return ins, ins
                    pg.op("act", fnh, reads=[uhT[j]], writes=[scrT[u]])
                    for (c0, n, kind) in blocks:
                        def fn(e, a=av[oi], sg=sg, u=u, c0=c0, n=n, kind=kind):
                            ins = e.tensor_tensor(out=scr[u][:, 30 + c0:30 + c0 + n], in0=scr[sg][:, c0:c0 + n],
                                                  in1=acc_ap(a, kind, n), op=ALU.mult)
                            return ins, ins
                        pg.op("dve", fn, reads=[accT[av[oi]], scrT[sg]], writes=[scrT[u]])

                    def fnh2(e, j=j, u=u):
                        ins = e.activation(out=uhist[:, j, :], in_=scr[u][:, ncol:ncol + 30], func=AF.Copy)
                        return ins, ins
                    pg.op("act", fnh2, reads=[scrT[u]], writes=[uhT[j]])
                yield ub
                cg = fullk("w_in", 6144 + p * 256, blocks, ncol, rhs_xn, xnT)
                bh = fullk("w_in", 8192 + p * 256, blocks, ncol, rhs_xn, xnT)
                for oi in range(2):
                    j = 2 * p + oi
                    th = oi
                    cbi = 6 + oi
                    for (c0, n, kind) in blocks:
                        def fn(e, a=bh[oi], th=th, c0=c0, n=n, kind=kind):
                            ins = e.activation(out=scr[th][:, c0:c0 + n], in_=acc_ap(a, kind, n), func=AF.Copy)
                            return ins, ins
                        pg.op("act", fn, reads=[accT[bh[oi]]], writes=[scrT[th]])

                    def fnh(e, j=j, cbi=cbi):
                        ins = e.activation(out=scr[cbi][:, 0:2], in_=cbhist[:, j, :], func=AF.Copy)
                        return ins, ins
                    pg.op("act", fnh, reads=[cbhT[j]], writes=[scrT[cbi]])
                    for (c0, n, kind) in blocks:
                        def fn(e, a=cg[oi], th=th, cbi=cbi, c0=c0, n=n, kind=kind):
                            ins = e.tensor_tensor(out=scr[cbi][:, 2 + c0:2 + c0 + n], in0=scr[th][:, c0:c0 + n],
                                                  in1=acc_ap(a, kind, n), op=ALU.mult)
                            return ins, ins
                        pg.op("dve", fn, reads=[accT[cg[oi]], scrT[th]], writes=[scrT[cbi]])

                    def fnh2(e, j=j, cbi=cbi):
                        ins = e.activation(out=cbhist[:, j, :], in_=scr[cbi][:, ncol:ncol + 2], func=AF.Copy)
                        return ins, ins
                    pg.op("act", fnh2, reads=[scrT[cbi]], writes=[cbhT[j]])
                yield None
                bg = fullk("w_in", 4096 + p * 256, blocks, ncol, rhs_xn, xnT)
                for oi in range(2):
                    j = 2 * p + oi
                    cbi = 6 + oi
                    vt = oi
                    col = P_CBW + j * 3

                    def fn(e, cbi=cbi, vt=vt, col=col):
                        first = e.tensor_scalar(out=scr[vt][:, 0:ncol], in0=scr[cbi][:, 0:ncol], scalar1=pc(col),
                                                scalar2=None, op0=ALU.mult)
                        return first, first
                    pg.op("dve", fn, reads=[scrT[cbi], parT], writes=[scrT[vt]])
                    for k in (1, 2):
                        def fn(e, cbi=cbi, vt=vt, col=col, k=k):
                            ins = e.scalar_tensor_tensor(out=scr[vt][:, 0:ncol], in0=scr[cbi][:, k:k + ncol],
                                                         scalar=pc(col + k), in1=scr[vt][:, 0:ncol],
                                                         op0=ALU.mult, op1=ALU.add)
                            return ins, ins
                        pg.op("dve", fn, reads=[scrT[cbi], parT], writes=[scrT[vt]])
                    g = j // 4
                    gi, kk = g % 2, 4 + j % 4
                    for (c0, n, kind) in blocks:
                        def fn(e, a=bg[oi], vt=vt, gi=gi, kk=kk, c0=c0, n=n, kind=kind):
                            ins = e.tensor_tensor(out=grp[gi][:, kk, c0:c0 + n], in0=scr[vt][:, c0:c0 + n],
                                                  in1=acc_ap(a, kind, n), op=ALU.mult)
                            return ins, ins
                        pg.op("dve", fn, reads=[accT[bg[oi]], scrT[vt]], writes=[grpT[gi][kk]])
                return

            def back_conv(p, ub, heads=(0, 1)):
                for oi in heads:
                    j = 2 * p + oi
                    u = ub[oi]
                    y = 8 + oi
                    colw = P_CAW + j * 31

                    def fn(e, u=u, y=y, colw=colw, j=j):
                        ins = e.tensor_scalar(out=scr[y][:, 0:ncol], in0=scr[u][:, 0:ncol], scalar1=pc(colw),
                                              scalar2=pc(P_CAB + j), op0=ALU.mult, op1=ALU.add)
                        return ins, ins
                    pg.op("dve", fn, reads=[scrT[u], parT], writes=[scrT[y]])
                    yb = 10 + oi

                    def fn(e, u=u, yb=yb, colw=colw):
                        ins = e.tensor_scalar(out=scr[yb][:, 0:ncol], in0=scr[u][:, 1:1 + ncol], scalar1=pc(colw + 1),
                                              scalar2=None, op0=ALU.mult)
                        return ins, ins
                    pg.op("dve", fn, reads=[scrT[u], parT], writes=[scrT[yb]])
                    for k in range(2, 31):
                        yy = y if k % 2 == 0 else yb

                        def fn(e, u=u, yy=yy, colw=colw, k=k):
                            ins = e.scalar_tensor_tensor(out=scr[yy][:, 0:ncol], in0=scr[u][:, k:k + ncol],
                                                         scalar=pc(colw + k), in1=scr[yy][:, 0:ncol],
                                                         op0=ALU.mult, op1=ALU.add)
                            return ins, ins
                        pg.op("dve", fn, reads=[scrT[u], parT], writes=[scrT[yy]])

                    def fn(e, y=y, yb=yb):
                        ins = e.tensor_tensor(out=scr[y][:, 0:ncol], in0=scr[y][:, 0:ncol], in1=scr[yb][:, 0:ncol],
                                              op=ALU.add)
                        return ins, ins
                    pg.op("dve", fn, reads=[scrT[yb]], writes=[scrT[y]])

            def back_ln(p):
                for oi in range(2):
                    j = 2 * p + oi
                    y = 8 + oi
                    yc = 10 + oi
                    am = nacc()

                    def fn(e, y=y, am=am):
                        fl = FL()
                        for (c0, n, kind) in blocks:
                            fl(e.matmul(acc_ap(am, kind, n), lhsT=onesF[:, :], rhs=scr[y][:, c0:c0 + n],
                                            start=True, stop=True))
                        return fl.f, fl.l
                    pg.op("pe", fn, reads=[scrT[y], constT], writes=[accT[am]])
                    for (c0, n, kind) in blocks:
                        def fn(e, y=y, yc=yc, am=am, c0=c0, n=n, kind=kind):
                            ins = e.tensor_tensor(out=scr[yc][:, c0:c0 + n], in0=scr[y][:, c0:c0 + n],
                                                  in1=acc_ap(am, kind, n), op=ALU.subtract)
                            return ins, ins
                        pg.op("dve", fn, reads=[accT[am], scrT[y]], writes=[scrT[yc]])

                    def fn(e, y=y, yc=yc):
                        ins = e.activation(out=scr[y][:, 0:ncol], in_=scr[yc][:, 0:ncol], func=AF.Square)
                        return ins, ins
                    pg.op("act", fn, reads=[scrT[yc]], writes=[scrT[y]])
                yield None
                for oi in range(2):
                    j = 2 * p + oi
                    y = 8 + oi
                    yc = 10 + oi
                    av_ = nacc()

                    def fn(e, y=y, av_=av_):
                        fl = FL()
                        for (c0, n, kind) in blocks:
                            fl(e.matmul(acc_ap(av_, kind, n), lhsT=onesF[:, :], rhs=scr[y][:, c0:c0 + n],
                                            start=True, stop=True))
                        return fl.f, fl.l
                    pg.op("pe", fn, reads=[scrT[y], constT], writes=[accT[av_]])
                    for (c0, n, kind) in blocks:
                        def fn(e, y=y, av_=av_, c0=c0, n=n, kind=kind):
                            ins = e.activation(out=scr[y][:, c0:c0 + n], in_=acc_ap(av_, kind, n), func=AF.Sqrt,
                                               bias=pc(P_EPS))
                            return ins, ins
                        pg.op("act", fn, reads=[accT[av_], parT], writes=[scrT[y]])

                        def fnb(e, y=y, c0=c0, n=n):
                            ins = e.reciprocal(out=scr[y][:, c0:c0 + n], in_=scr[y][:, c0:c0 + n])
                            return ins, ins
                        pg.op("dve", fnb, reads=[], writes=[scrT[y]])

                    def fn(e, y=y, yc=yc):
                        ins = e.tensor_tensor(out=scr[yc][:, 0:ncol], in0=scr[yc][:, 0:ncol], in1=scr[y][:, 0:ncol],
                                              op=ALU.mult)
                        return ins, ins
                    pg.op("dve", fn, reads=[scrT[y]], writes=[scrT[yc]])
                    g = j // 4
                    gi, kk = g % 2, j % 4

                    def fn(e, yc=yc, gi=gi, kk=kk, j=j):
                        ins = e.activation(out=grp[gi][:, kk, 0:ncol], in_=scr[yc][:, 0:ncol], func=AF.Silu,
                                           scale=pc(P_LNG + j), bias=pc(P_LNB + j))
                        return ins, ins
                    pg.op("act", fn, reads=[scrT[yc], parT], writes=[grpT[gi][kk]])

            prev = None
            for p in range(9):
                fr = front(p) if p < 8 else None
                ub = None
                if prev is not None and fr is not None:
                    back_conv(prev[0], prev[1], heads=(0,))
                    ub = next(fr)
                    back_conv(prev[0], prev[1], heads=(1,))
                elif prev is not None:
                    back_conv(prev[0], prev[1], heads=(0,))
                    partial_proj("w_out", [(4 * 2, 4), (16 + 4 * 2, 4)], 2 % 2, 8, blocks, obs=range(0, 4))
                    back_conv(prev[0], prev[1], heads=(1,))
                    partial_proj("w_out", [(4 * 2, 4), (16 + 4 * 2, 4)], 2 % 2, 8, blocks, obs=range(4, 8))
                else:
                    ub = next(fr)
                if p in (4, 6):
                    g = (p - 4) // 2
                    partial_proj("w_out", [(4 * g, 4), (16 + 4 * g, 4)], g % 2, 8, blocks)
                if fr is not None:
                    next(fr)
                bl = back_ln(prev[0]) if prev is not None else None
                if bl is not None:
                    next(bl)
                if fr is not None:
                    next(fr, None)
                if bl is not None:
                    next(bl, None)
                prev = (p, ub) if p < 8 else None
            partial_proj("w_out", [(4 * 3, 4), (16 + 4 * 3, 4)], 3 % 2, 8, blocks)

        if upto >= 2:
            rmsnorm(P_GX, blocks, ncol)
            rhs_xn = lambda k, c0, n: xn[:, k, c0:c0 + n]
            tchunks = []
            if ti == 0:
                tchunks.append((0, HALO, "h", 0))
            for tc in range(4):
                tchunks.append((mainc0 + tc * 128, 128, "m", tc * 128))

            def do_q(hd):
                gi = 0
                for cbk in range(4):
                    accs = fullk("w_q", hd * 1024 + cbk * 256, blocks, ncol, rhs_xn, xnT)
                    for oi in range(2):
                        kk = cbk * 2 + oi
                        for (c0, n, kind) in blocks:
                            def fn(e, a=accs[oi], kk=kk, c0=c0, n=n, kind=kind):
                                ins = e.activation(out=grp[gi][:, kk, c0:c0 + n], in_=acc_ap(a, kind, n), func=AF.Copy)
                                return ins, ins
                            pg.op("act", fn, reads=[accT[accs[oi]]], writes=[grpT[gi][kk]])

            def do_s(hd):
                gi = 0

                def fk(e, hd=hd):
                    ins = e.dma_start(out=ktb[:, :, :], in_=kt_scr[hd].rearrange("p (k m) -> p k m", m=MEM))
                    return ins, ins
                pg.dma("sp", fk, ktT, reads=(ktscrT[hd],), writes=(ktT,))
                apt = nacc()
                pend_pt = None
                first_pt = True
                for ci, (c0, n, kind, pcol) in enumerate(tchunks):
                    s_list = [a_ for a_ in range(cur["nacc"]) if a_ != apt]
                    asb = s_list[ci % len(s_list)]
                    soff = 0

                    def fs(e, c0=c0, n=n, asb=asb, soff=soff):
                        fl = FL()
                        for c in range(8):
                            fl(e.matmul(ps[0:n, asb, soff:soff + MEM], lhsT=grp[gi][:, c, c0:c0 + n],
                                            rhs=ktb[:, c, :], start=(c == 0), stop=(c == 7)))
                        return fl.f, fl.l
                    pg.op("pe", fs, reads=grpT[gi] + [ktT], writes=[accT[asb]])
                    sm = nxt("sm", 4)
                    pt = nxt("pt", 2)

                    def f1(e, n=n, asb=asb, soff=soff, sm=sm):
                        ins = e.reduce_max(out=small[sm][0:n, 0:1], in_=ps[0:n, asb, soff:soff + MEM], axis=AX.X)
                        return ins, ins
                    pg.op("dve", f1, reads=[accT[asb]], writes=[smallT[sm]])

                    def f2(e, n=n, sm=sm):
                        ins = e.tensor_scalar(out=small[sm][0:n, 1:2], in0=small[sm][0:n, 0:1], scalar1=-1.0 / 32.0,
                                              scalar2=None, op0=ALU.mult)
                        return ins, ins
                    pg.op("dve", f2, reads=[], writes=[smallT[sm]])

                    def f3(e, n=n, asb=asb, soff=soff, sm=sm, pt=pt):
                        ins = e.activation(out=Pt[pt][0:n, :], in_=ps[0:n, asb, soff:soff + MEM], func=AF.Exp,
                                           bias=small[sm][0:n, 1:2], scale=1.0 / 32.0, accum_out=small[sm][0:n, 2:3])
                        return ins, ins
                    pg.op("act", f3, reads=[accT[asb], smallT[sm]], writes=[PtT[pt], smallT[sm]])

                    def f4(e, n=n, sm=sm):
                        ins = e.reciprocal(out=small[sm][0:n, 3:4], in_=small[sm][0:n, 2:3])
                        return ins, ins
                    pg.op("dve", f4, reads=[], writes=[smallT[sm]])

                    def f5(e, n=n, sm=sm, pt=pt):
                        ins = e.tensor_scalar(out=Pt[pt][0:n, :], in0=Pt[pt][0:n, :], scalar1=small[sm][0:n, 3:4],
                                              scalar2=None, op0=ALU.mult)
                        return ins, ins
                    pg.op("dve", f5, reads=[smallT[sm]], writes=[PtT[pt]])

                    def f6(e, n=n, kind=kind, pcol=pcol, pt=pt, apt=apt):
                        fl = FL()
                        for mc in range(2):
                            if kind == "h":
                                dst = ps[:, 4 + apt, 0:32].bitcast(BF16)[:, mc * 32:mc * 32 + n]
                            else:
                                dst = ps[:, apt, :].bitcast(BF16)[:, mc * 512 + pcol:mc * 512 + pcol + n]
                            fl(e.transpose(dst, Pt[pt][0:n, mc * 128:(mc + 1) * 128], identB[0:n, 0:n]))
                        return fl.f, fl.l
                    if pend_pt is not None:
                        pg.op("pe", pend_pt[0], reads=[PtT[pend_pt[1]], constT], writes=[accT[apt]])
                    pend_pt = (f6, pt)
                pg.op("pe", pend_pt[0], reads=[PtT[pend_pt[1]], constT], writes=[accT[apt]])
                for (c0, n, kind) in blocks:
                    def f7(e, c0=c0, n=n, kind=kind, apt=apt):
                        if kind == "h":
                            src = ps[:, 4 + apt, 0:32].bitcast(BF16).rearrange("p (m t) -> p m t", t=32)
                        else:
                            src = ps[:, apt, :].bitcast(BF16).rearrange("p (m t) -> p m t", t=512)
                        ins = e.activation(out=PTsb[:, :, c0:c0 + n], in_=src, func=AF.Copy)
                        return ins, ins
                    pg.op("act", f7, reads=[accT[apt]], writes=[PTsbT])

            def do_pv(hd):
                gi = 1

                def fv(e, hd=hd):
                    ins = e.dma_start(out=vb[:, :, :], in_=v_scr[hd].rearrange("p (k m) -> p k m", m=1024))
                    return ins, ins
                pg.dma("sp", fv, vT, reads=(vscrT[hd],), writes=(vT,))
                for c in range(8):
                    a = nacc()

                    def fp(e, c=c, a=a):
                        fl = FL()
                        for mc in range(2):
                            for (c0, n, kind) in blocks:
                                fl(e.matmul(acc_ap(a, kind, n), lhsT=vb[:, mc, c * 128:(c + 1) * 128],
                                                rhs=PTsb[:, mc, c0:c0 + n], start=(mc == 0), stop=(mc == 1)))
                        return fl.f, fl.l
                    pg.op("pe", fp, reads=[vT, PTsbT], writes=[accT[a]])
                    for (c0, n, kind) in blocks:
                        def fe(e, c=c, a=a, c0=c0, n=n, kind=kind):
                            ins = e.activation(out=grp[gi][:, c, c0:c0 + n], in_=acc_ap(a, kind, n), func=AF.Copy)
                            return ins, ins
                        pg.op("act", fe, reads=[accT[a]], writes=[grpT[gi][c]])
                partial_proj("w_o", [(hd * 8, 8)], gi, 8, blocks)

            do_q(0)
            do_s(0)
            for hd in range(4):
                if hd + 1 < 4:
                    do_q(hd + 1)
                do_pv(hd)
                if hd + 1 < 4:
                    do_s(hd + 1)

        if upto >= 3:
            rmsnorm(P_GF, blocks, ncol)
            rhs_xn = lambda k, c0, n: xn[:, k, c0:c0 + n]
            mblk = [blocks[-1]]
            for fp_ in range(FC // 2):
                gacc = fullk("w_gate", fp_ * 256, blocks, ncol, rhs_xn, xnT)
                uacc = fullk("w_up", fp_ * 256, mblk, ncol, rhs_xn, xnT)
                for oi in range(2):
                    fc = fp_ * 2 + oi
                    fg = fc // 8
                    gi, kk = fg % 2, fc % 8
                    gb = fc % 3
                    gc = 3 + fc % 3

                    def fh(e, fc=fc, gb=gb):
                        ins = e.activation(out=scr[gb][:, 0:2], in_=ghist[:, fc, :], func=AF.Copy)
                        return ins, ins
                    pg.op("act", fh, reads=[ghT[fc]], writes=[scrT[gb]])
                    for (c0, n, kind) in blocks:
                        def fn(e, a=gacc[oi], gb=gb, c0=c0, n=n, kind=kind):
                            if kind == "h":
                                ins = e.activation(out=scr[gb][:, 2 + c0:2 + c0 + n], in_=acc_ap(a, kind, n),
                                                   func=AF.Identity, scale=pc(P_MASK))
                            else:
                                ins = e.activation(out=scr[gb][:, 2 + c0:2 + c0 + n], in_=acc_ap(a, kind, n),
                                                   func=AF.Copy)
                            return ins, ins
                        pg.op("act", fn, reads=[accT[gacc[oi]], parT], writes=[scrT[gb]])

                    def fh2(e, fc=fc, gb=gb):
                        ins = e.activation(out=ghist[:, fc, :], in_=scr[gb][:, ncol:ncol + 2], func=AF.Copy)
                        return ins, ins
                    pg.op("act", fh2, reads=[scrT[gb]], writes=[ghT[fc]])
                    col = P_CFW + fc * 3

                    def fn(e, gb=gb, gc=gc, col=col):
                        ins = e.tensor_scalar(out=scr[gc][:, 0:ncol], in0=scr[gb][:, 0:ncol], scalar1=pc(col),
                                              scalar2=None, op0=ALU.mult)
                        return ins, ins
                    pg.op("dve", fn, reads=[scrT[gb], parT], writes=[scrT[gc]])
                    for k in (1, 2):
                        def fn(e, gb=gb, gc=gc, col=col, k=k):
                            ins = e.scalar_tensor_tensor(out=scr[gc][:, 0:ncol], in0=scr[gb][:, k:k + ncol],
                                                         scalar=pc(col + k), in1=scr[gc][:, 0:ncol],
                                                         op0=ALU.mult, op1=ALU.add)
                            return ins, ins
                        pg.op("dve", fn, reads=[scrT[gb], parT], writes=[scrT[gc]])

                    def fn(e, gb=gb, gc=gc):
                        ins = e.activation(out=scr[gb][:, 0:ncol], in_=scr[gc][:, 0:ncol], func=AF.Silu)
                        return ins, ins
                    pg.op("act", fn, reads=[scrT[gc]], writes=[scrT[gb]])
                    for (c0, n, kind) in mblk:
                        def fn(e, a=uacc[oi], gb=gb, gi=gi, kk=kk, c0=c0, n=n, kind=kind):
                            ins = e.tensor_tensor(out=grp[gi][:, kk, c0:c0 + n], in0=scr[gb][:, c0:c0 + n],
                                                  in1=acc_ap(a, kind, n), op=ALU.mult)
                            return ins, ins
                        pg.op("dve", fn, reads=[accT[uacc[oi]], scrT[gb]], writes=[grpT[gi][kk]])
                    if kk == 7 or fc == FC - 1:
                        nk = kk + 1
                        partial_proj("w_down", [(fg * 8, nk)], gi, nk, mblk)

        if upto >= 4:
            rmsnorm(P_GFIN, blocks, ncol, final=True)
        nxt_tile = ti + 1 < ntiles
        a_next = None
        if nxt_tile and upto >= 1:
            a_next = nacc()
            cur["hold"].add(a_next)
        nblocks = [(0, NTM, "m")]
        for db in range(8):
            for tc in range(4):
                col0 = mainc0 + tc * 128
                orow = ti * NTM + tc * 128
                a = nacc()

                def ft(e, db=db, a=a, col0=col0):
                    fl = FL()
                    for i in range(4):
                        fl(e.transpose(ps[:, a, i * 128:(i + 1) * 128], h[:, db * 4 + i, col0:col0 + 128],
                                       par[:, P_ID:P_ID + 128]))
                    return fl.f, fl.l
                pg.op("pe", ft, reads=hT[db * 4:(db + 1) * 4] + [constT], writes=[accT[a]])
                si = nxt("stg", NSTG + 1)

                def fe(e, a=a, si=si):
                    ins = e.activation(out=stg[si][:, 0:512], in_=ps[:, a, :], func=AF.Copy)
                    return ins, ins
                pg.op("act", fe, reads=[accT[a]], writes=[stgT[si]])

                def fo(e, si=si, orow=orow, db=db):
                    ins = e.dma_start(out=out_d[orow:orow + 128, db * 512:(db + 1) * 512], in_=stg[si][:, 0:512])
                    return ins, ins
                pg.dma("sp", fo, stgT[si], reads=(stgT[si],), writes=())
            if nxt_tile:
                nrow0 = HALO + (ti + 1) * NTM
                for tc in range(4):
                    r = nrow0 + tc * 128
                    load_unit(xin[r:r + 128, :], 128, tc * 128, db)
                if a_next is not None:
                    for c in range(db * 4, db * 4 + 4):
                        rms_stat_chunk(a_next, c, nblocks, NTM)
        if a_next is not None:
            cur["pre_acc"] = a_next

    for ti_ in range(ntiles):
        do_tile(ti_)

    pg.wait_all("sp", stgT)
    pg.emit(nc, es)
    print('sbuf bytes remaining', nc.sbuf_bytes_remaining)
    es.close()
    return nc


def pack_params(inp, half):
    P = np.zeros((128, NPAR), np.float32)

    def pm(v):
        return np.ascontiguousarray(v.reshape(-1, 128).T)
    P[:, P_GMIX:P_GMIX + 32] = pm(inp["g_mix"][0])
    P[:, P_GX:P_GX + 32] = pm(inp["g_xattn"][0])
    P[:, P_GF:P_GF + 32] = pm(inp["g_ffn"][0])
    P[:, P_GFIN:P_GFIN + 32] = pm(inp["g_final"])
    P[:, P_GMEM:P_GMEM + 32] = pm(inp["g_mem"])
    caw = inp["conv_a_w"][0]
    P[:, P_CAW:P_CAW + 496] = caw.reshape(31, 16, 128).transpose(2, 1, 0).reshape(128, 496)
    P[:, P_CAB:P_CAB + 16] = pm(inp["conv_a_b"][0])
    P[:, P_LNG:P_LNG + 16] = pm(inp["ln_a_g"][0])
    P[:, P_LNB:P_LNB + 16] = pm(inp["ln_a_b"][0])
    cbw = inp["conv_b_w"][0]
    P[:, P_CBW:P_CBW + 48] = cbw.reshape(3, 16, 128).transpose(2, 1, 0).reshape(128, 48)
    cfw = inp["conv_f_w"][0]
    P[:, P_CFW:P_CFW + FC * 3] = cfw.reshape(3, FC, 128).transpose(2, 1, 0).reshape(128, FC * 3)
    P[:, P_MASK] = float(half)
    P[:, P_EPS] = EPS
    P[:, P_ID:P_ID + 128] = np.eye(128, dtype=np.float32)
    return P


_NC_CACHE = {}


def make_in_maps(inp, ncores=8):
    x = np.asarray(inp["x"], np.float32)
    mem = np.asarray(inp["mem"], np.float32)
    wts = {k: np.ascontiguousarray(np.asarray(inp[k], np.float32)[0]) for k in
           ("w_in", "w_out", "w_q", "w_k", "w_v", "w_o", "w_gate", "w_up", "w_down")}
    inp = {k: np.asarray(v, np.float32) for k, v in inp.items() if k not in ("x", "mem") and k not in wts}
    maps = []
    for c in range(ncores):
        b, half = c // 2, c % 2
        t0 = half * TOK_CORE
        xi = np.zeros((HALO + TOK_CORE, D), np.float32)
        if half == 0:
            xi[HALO:] = x[b, 0:TOK_CORE]
        else:
            xi[:] = x[b, t0 - HALO:t0 + TOK_CORE]
        m = {"xin": xi, "memin": np.ascontiguousarray(mem[b]), "params": pack_params(inp, half)}
        m.update(wts)
        maps.append(m)
    return maps


def kernel(**inputs):
    key = "full"
    if key not in _NC_CACHE:
        _NC_CACHE[key] = build()
    nc = _NC_CACHE[key]
    maps = make_in_maps(inputs, 8)
    res = run_bass_kernel_spmd(nc, maps, core_ids=list(range(8)))
    out = np.zeros((4, SEQ, D), np.float32)
    for c in range(8):
        b, half = c // 2, c % 2
        out[b, half * TOK_CORE:(half + 1) * TOK_CORE] = res.results[c]["out"]
    return out
```
